# Optimizing a Trainium2 kernel written in Bass

```python
import math
import jax
import jax.numpy as jnp
from jax import lax
import numpy as np

D_MODEL = 1024
BATCH = 4
SEQ = 8192
DEPTH = 4

CHUNK = 64
N_MIXERS = 3
HEAD_DIM = 64
N_HEADS = D_MODEL // HEAD_DIM
D_FF = -(-8 * D_MODEL // (3 * 256)) * 256
RMS_EPS = 1e-6
ROPE_THETA = 10000.0
RW_DECAY_LORA = max(32, int(round(1.8 * D_MODEL ** 0.5 / 32)) * 32)
RW_AAA_LORA = max(32, int(round(1.8 * D_MODEL ** 0.5 / 32)) * 32)
RW_MV_LORA = max(32, int(round(1.3 * D_MODEL ** 0.5 / 32)) * 32)
RW_GATE_LORA = max(32, int(round(0.6 * D_MODEL ** 0.8 / 32)) * 32)
RW_GN_EPS = HEAD_DIM * 1e-5
SB_QBLOCK = 128
DS_TOPK_MAX = 256
DS_IDX_HEADS = 8
DS_IDX_DIM = 64
DS_QBLOCK = CHUNK
DS_IN_WIDTH = 3 * D_MODEL + DS_IDX_HEADS * DS_IDX_DIM + DS_IDX_DIM + DS_IDX_HEADS
N_RWKV = (DEPTH + 2) // 3
N_SB = (DEPTH + 1) // 3
N_DSA = DEPTH // 3

kernel_name = 'hybrid_rwkv7_stickbreak_dsa'


def rms_norm(x, g):
    xf = x.astype(jnp.float32)
    y = xf * lax.rsqrt(jnp.mean(xf * xf, axis=-1, keepdims=True) + RMS_EPS)
    return (y * g.astype(jnp.float32)).astype(x.dtype)


def rope(x, pos):
    half = x.shape[-1] // 2
    inv = 1.0 / (ROPE_THETA ** (jnp.arange(half, dtype=jnp.float32) / half))
    ang = pos.astype(jnp.float32)[:, None] * inv[None, :]
    cos = jnp.cos(ang)[None, :, None, :]
    sin = jnp.sin(ang)[None, :, None, :]
    xf = x.astype(jnp.float32)
    x1, x2 = xf[..., :half], xf[..., half:]
    return jnp.concatenate([x1 * cos - x2 * sin, x1 * sin + x2 * cos], axis=-1).astype(x.dtype)


def swiglu(h, w_gu, w_down):
    g, u = jnp.split(h @ w_gu, 2, axis=-1)
    return (jax.nn.silu(g) * u) @ w_down


def rwkv7_time_mix(h, v_first, mix, w_rkv, w0, w1, w2, a0, a1, a2, g1, g2, k_k, k_a, r_k, ln_w, ln_b, w_out, v_res):
    B, S, D = h.shape
    f32 = jnp.float32
    shifted = jnp.pad(h, ((0, 0), (1, 0), (0, 0)))[:, :-1]
    xm = h[:, :, None, :] + (shifted - h)[:, :, None, :] * mix
    rkv = jnp.einsum('bsjd,jde->bsje', xm[:, :, :3], w_rkv)
    r, k, v = rkv[:, :, 0], rkv[:, :, 1], rkv[:, :, 2]
    xv, xw, xa, xg = xm[:, :, 2], xm[:, :, 3], xm[:, :, 4], xm[:, :, 5]
    logw = -jax.nn.softplus(-(w0 + jnp.tanh(xw @ w1) @ w2).astype(f32)) - 0.5
    decay = jnp.exp(-jnp.exp(logw))
    a = jax.nn.sigmoid((a0 + (xa @ a1) @ a2).astype(f32))
    g = jax.nn.sigmoid(xg @ g1) @ g2
    v_layer = v
    if v_res is not None:
        v0, v1, v2 = v_res
        v = v + (v_first - v) * jax.nn.sigmoid(v0 + (xv @ v1) @ v2)

    def heads(t):
        return t.astype(f32).reshape(B, S, N_HEADS, HEAD_DIM)

    kk = heads(k * k_k)
    kk = kk * lax.rsqrt(jnp.maximum(jnp.sum(kk * kk, axis=-1, keepdims=True), 1e-24))
    k = k.astype(f32) * (1.0 + (a - 1.0) * k_a)
    rh, kh, vh, wh, ah = heads(r), heads(k), heads(v), heads(decay), heads(a)

    def step(state, inp):
        r_t, w_t, k_t, v_t, kk_t, a_t = inp
        sa = jnp.einsum('bhvk,bhk->bhv', state, kk_t)
        state = (state * w_t[:, :, None, :]
                 - sa[..., None] * (kk_t * a_t)[:, :, None, :]
                 + v_t[..., None] * k_t[:, :, None, :])
        return state, jnp.einsum('bhvk,bhk->bhv', state, r_t)

    def tmaj(t):
        return jnp.moveaxis(t, 1, 0)

    s0 = jnp.zeros((B, N_HEADS, HEAD_DIM, HEAD_DIM), f32)
    _, y = lax.scan(step, s0, (tmaj(rh), tmaj(wh), tmaj(kh), tmaj(vh), tmaj(kk), tmaj(ah)))
    y = jnp.moveaxis(y, 0, 1)
    mu = jnp.mean(y, axis=-1, keepdims=True)
    var = jnp.mean(jnp.square(y - mu), axis=-1, keepdims=True)
    y = ((y - mu) * lax.rsqrt(var + RW_GN_EPS)).reshape(B, S, D) * ln_w + ln_b
    y = y + (jnp.sum(rh * kh * r_k, axis=-1, keepdims=True) * vh).reshape(B, S, D)
    out = (y * g).astype(h.dtype) @ w_out
    return out, v_layer


def stick_breaking_attention(h, w_qkv, w_out):
    B, S, D = h.shape
    f32 = jnp.float32
    qkv = (h @ w_qkv).reshape(B, S, 3, N_HEADS, HEAD_DIM)
    q = qkv[:, :, 0].astype(f32) * HEAD_DIM ** -0.5
    k = qkv[:, :, 1].astype(f32)
    v = qkv[:, :, 2]
    nb = S // SB_QBLOCK
    q_blocks = jnp.moveaxis(q.reshape(B, nb, SB_QBLOCK, N_HEADS, HEAD_DIM), 1, 0)
    kpos = jnp.arange(S)

    def block(args):
        q_b, i = args
        qpos = i * SB_QBLOCK + jnp.arange(SB_QBLOCK)
        earlier = kpos[None, :] < qpos[:, None]
        z = jnp.einsum('bqhd,bshd->bhqs', q_b, k)
        log_1m = jnp.where(earlier, jax.nn.log_sigmoid(-z), 0.0)
        between = lax.cumsum(log_1m, axis=3, reverse=True) - log_1m
        att = jnp.where(earlier, jnp.exp(jax.nn.log_sigmoid(z) + between), 0.0)
        return jnp.einsum('bhqs,bshd->bqhd', att.astype(v.dtype), v)

    o = lax.map(block, (q_blocks, jnp.arange(nb)))
    o = jnp.moveaxis(o, 0, 1).reshape(B, S, D)
    return o @ w_out


def dsa_attention(h, w_in, q_norm, k_norm, w_out):
    B, S, D = h.shape
    f32 = jnp.float32
    pos = jnp.arange(S)
    c3 = 3 * D
    c4 = c3 + DS_IDX_HEADS * DS_IDX_DIM
    c5 = c4 + DS_IDX_DIM
    q, k, v, qi, ki, wi = jnp.split(h @ w_in, [D, 2 * D, c3, c4, c5], axis=-1)
    q = rope(rms_norm(q.reshape(B, S, N_HEADS, HEAD_DIM), q_norm), pos)
    k = rope(rms_norm(k.reshape(B, S, N_HEADS, HEAD_DIM), k_norm), pos)
    v = v.reshape(B, S, N_HEADS, HEAD_DIM)
    qi = rope(qi.reshape(B, S, DS_IDX_HEADS, DS_IDX_DIM), pos)
    ki = rope(ki.reshape(B, S, 1, DS_IDX_DIM), pos)[:, :, 0]
    wi = wi * (DS_IDX_HEADS ** -0.5 * DS_IDX_DIM ** -0.5)
    n_sel = min(DS_TOPK_MAX, S // 4)
    nb = S // DS_QBLOCK

    def blocks(t):
        return jnp.moveaxis(t.reshape(B, nb, DS_QBLOCK, *t.shape[2:]), 1, 0)

    kpos = jnp.arange(S)

    def block(args):
        q_b, qi_b, wi_b, i = args
        qpos = i * DS_QBLOCK + jnp.arange(DS_QBLOCK)
        visible = kpos[None, :] < ((qpos // CHUNK + 1) * CHUNK)[:, None]
        idx = jnp.einsum('bqhd,bsd->bqhs', qi_b, ki)
        score = jnp.einsum('bqh,bqhs->bqs', wi_b, jax.nn.relu(idx)).astype(f32)
        score = jnp.where(visible[None], score, -jnp.inf)
        top_val, top_idx = lax.top_k(score, n_sel)
        valid = top_val > -jnp.inf
        k_sel = jax.vmap(lambda kb, ib: kb[ib])(k, top_idx)
        v_sel = jax.vmap(lambda vb, ib: vb[ib])(v, top_idx)
        logits = jnp.einsum('bqhd,bqkhd->bhqk', q_b, k_sel).astype(f32) * HEAD_DIM ** -0.5
        logits = jnp.where(valid[:, None], logits, -jnp.inf)
        p = jax.nn.softmax(logits, axis=-1)
        return jnp.einsum('bhqk,bqkhd->bqhd', p.astype(v.dtype), v_sel)

    o = lax.map(block, (blocks(q), blocks(qi), blocks(wi), jnp.arange(nb)))
    o = jnp.moveaxis(o, 0, 1).reshape(B, S, D)
    return o @ w_out


def setup_inputs(seed: int = 0) -> dict:
    key = jax.random.key(seed)
    ks = iter(jax.random.split(key, 64))
    D, F, H, dh = D_MODEL, D_FF, N_HEADS, HEAD_DIM
    nA, nB, nC = N_RWKV, N_SB, N_DSA
    out_scale = D ** -0.5 / math.sqrt(2 * DEPTH)

    def nrm(shape, scale):
        return jax.random.normal(next(ks), shape, jnp.float32) * scale

    def unif(shape, lo, hi):
        return jax.random.uniform(next(ks), shape, jnp.float32, lo, hi)

    return {
        'x': nrm((BATCH, SEQ, D), 1.0),
        'norm_mix': 1.0 + nrm((DEPTH, D), 0.02),
        'norm_ffn': 1.0 + nrm((DEPTH, D), 0.02),
        'ffn_w_gu': nrm((DEPTH, D, 2 * F), D ** -0.5),
        'ffn_w_down': nrm((DEPTH, F, D), F ** -0.5 / math.sqrt(2 * DEPTH)),
        'rw_mix': unif((nA, 6, D), 0.0, 1.0),
        'rw_w_rkv': nrm((nA, 3, D, D), D ** -0.5),
        'rw_w0': unif((nA, D), -5.0, 1.0),
        'rw_w1': nrm((nA, D, RW_DECAY_LORA), D ** -0.5),
        'rw_w2': nrm((nA, RW_DECAY_LORA, D), 0.1 * RW_DECAY_LORA ** -0.5),
        'rw_a0': nrm((nA, D), 0.1),
        'rw_a1': nrm((nA, D, RW_AAA_LORA), D ** -0.5),
        'rw_a2': nrm((nA, RW_AAA_LORA, D), 0.3 * RW_AAA_LORA ** -0.5),
        'rw_g1': nrm((nA, D, RW_GATE_LORA), D ** -0.5),
        'rw_g2': nrm((nA, RW_GATE_LORA, D), RW_GATE_LORA ** -0.5),
        'rw_v0': 1.0 + nrm((nA - 1, D), 0.1),
        'rw_v1': nrm((nA - 1, D, RW_MV_LORA), D ** -0.5),
        'rw_v2': nrm((nA - 1, RW_MV_LORA, D), 0.3 * RW_MV_LORA ** -0.5),
        'rw_k_k': 0.85 + nrm((nA, D), 0.05),
        'rw_k_a': 1.0 + nrm((nA, D), 0.05),
        'rw_r_k': nrm((nA, H, dh), 0.1),
        'rw_ln_w': 1.0 + nrm((nA, D), 0.02),
        'rw_ln_b': nrm((nA, D), 0.02),
        'rw_w_out': nrm((nA, D, D), out_scale),
        'sb_w_qkv': nrm((nB, D, 3 * D), D ** -0.5),
        'sb_w_out': nrm((nB, D, D), out_scale),
        'ds_w_in': nrm((nC, D, DS_IN_WIDTH), D ** -0.5),
        'ds_q_norm': 1.0 + nrm((nC, dh), 0.02),
        'ds_k_norm': 1.0 + nrm((nC, dh), 0.02),
        'ds_w_out': nrm((nC, D, D), out_scale),
    }


def reference(x, norm_mix, norm_ffn, ffn_w_gu, ffn_w_down,
              rw_mix, rw_w_rkv, rw_w0, rw_w1, rw_w2, rw_a0, rw_a1, rw_a2, rw_g1, rw_g2,
              rw_v0, rw_v1, rw_v2, rw_k_k, rw_k_a, rw_r_k, rw_ln_w, rw_ln_b, rw_w_out,
              sb_w_qkv, sb_w_out,
              ds_w_in, ds_q_norm, ds_k_norm, ds_w_out):
    v_first = None
    for i in range(DEPTH):
        kind, j = i % N_MIXERS, i // N_MIXERS
        h = rms_norm(x, norm_mix[i])
        if kind == 0:
            v_res = None if j == 0 else (rw_v0[j - 1], rw_v1[j - 1], rw_v2[j - 1])
            y, v_layer = rwkv7_time_mix(h, v_first, rw_mix[j], rw_w_rkv[j], rw_w0[j], rw_w1[j], rw_w2[j],
                                        rw_a0[j], rw_a1[j], rw_a2[j], rw_g1[j], rw_g2[j],
                                        rw_k_k[j], rw_k_a[j], rw_r_k[j], rw_ln_w[j], rw_ln_b[j],
                                        rw_w_out[j], v_res)
            if j == 0:
                v_first = v_layer
        elif kind == 1:
            y = stick_breaking_attention(h, sb_w_qkv[j], sb_w_out[j])
        else:
            y = dsa_attention(h, ds_w_in[j], ds_q_norm[j], ds_k_norm[j], ds_w_out[j])
        x = x + y
        x = x + swiglu(rms_norm(x, norm_ffn[i]), ffn_w_gu[i], ffn_w_down[i])
    return x
```

```python
from contextlib import ExitStack
import numpy as np
import ml_dtypes
import concourse.bass as bass
import concourse.mybir as mybir
from concourse.bass_utils import run_bass_kernel_spmd

F32 = mybir.dt.float32
BF16 = mybir.dt.bfloat16
AF = mybir.ActivationFunctionType
ALU = mybir.AluOpType
AX = mybir.AxisListType

D = 1024
NH = 16
DH = 64
FF = 2816
EPS = 1e-6
GN_EPS = 64 * 1e-5
TG = 512
NEG = -1.0e30


class V:
    __slots__ = ("bufs", "ap")

    def __init__(self, bufs, ap):
        self.bufs = bufs
        self.ap = ap


class Buf:
    __slots__ = ("name", "w", "r", "ap", "regs")

    def __init__(self, name, ap=None):
        self.name = name
        self.w = None
        self.r = {}
        self.ap = ap
        self.regs = {}

    def __getitem__(self, idx):
        return V((self,), self.ap[idx])

    def v(self, ap):
        return V((self,), ap)

    def reg(self, key):
        b = self.regs.get(key)
        if b is None:
            b = Buf("%s/%s" % (self.name, key), self.ap)
            self.regs[key] = b
        return b

    def rv(self, keys, ap):
        return V(tuple(self.reg(k) for k in keys), ap)


def _bufs(vs):
    out = []
    for v in vs:
        if v is None or isinstance(v, (int, float)):
            continue
        out.extend(v.bufs)
    return out


class K:
    def __init__(self, nc, n_dma_sems=8):
        self.nc = nc
        self.base = ExitStack()
        self.stack = self.base
        self.eng = {"pe": nc.tensor, "act": nc.scalar, "dve": nc.vector,
                    "pool": nc.gpsimd, "sp": nc.sync}
        self.sems = {}
        self.cnt = {}
        self.seen = {e: {} for e in self.eng}
        for e in self.eng:
            self.sems[e] = self.base.enter_context(nc.semaphore("c_" + e))
            self.cnt[e] = 0
        self.dsem = {}
        self.dcnt = {}
        self.dptr = {}
        for q in ("sp", "pool"):
            self.dsem[q] = []
            for i in range(n_dma_sems):
                key = "d_%s%d" % (q, i)
                self.sems[key] = self.base.enter_context(nc.semaphore(key))
                self.dsem[q].append(key)
                self.dcnt[key] = 0
            self.dptr[q] = 0
        self.ninst = 0
        self.nwait = 0
        self.uid = 0

    def sb(self, name, shape, dt=F32):
        self.uid += 1
        nm = "%s_%d" % (name, self.uid)
        t = self.stack.enter_context(self.nc.sbuf_tensor(nm, list(shape), dt))
        ap = t[:]
        assert tuple(ap.shape) == tuple(shape), (ap.shape, shape)
        return Buf(nm, ap)

    def ps(self, name, shape, dt=F32):
        self.uid += 1
        nm = "%s_%d" % (name, self.uid)
        t = self.stack.enter_context(self.nc.psum_tensor(nm, list(shape), dt))
        ap = t[:]
        assert tuple(ap.shape) == tuple(shape), (ap.shape, shape)
        return Buf(nm, ap)

    def dram(self, name, shape, dt=F32, kind="Internal"):
        t = self.nc.dram_tensor(name, list(shape), dt, kind=kind)
        return Buf(name, t.ap())

    def _wait(self, e, ev):
        key, val = ev
        if self.seen[e].get(key, 0) >= val:
            return
        self.eng[e].wait_ge(self.sems[key], val)
        self.seen[e][key] = val
        self.nwait += 1

    def _deps(self, e, reads, writes, pe_accum=False):
        for b in reads:
            if b.w is not None:
                self._wait(e, b.w)
        for b in writes:
            if b.w is not None and not (pe_accum and b.w[0] == "pe"):
                self._wait(e, b.w)
            for key, val in b.r.items():
                self._wait(e, (key, val))

    def _mark(self, ev, reads, writes):
        for b in reads:
            if b.r.get(ev[0], 0) < ev[1]:
                b.r[ev[0]] = ev[1]
        for b in writes:
            b.w = ev
            b.r = {}

    def op(self, e, issue, reads=(), writes=(), pe_accum=False):
        reads = _bufs(reads)
        writes = _bufs(writes)
        self._deps(e, reads, writes, pe_accum)
        inst = issue(self.eng[e])
        self.cnt[e] += 1
        inst.then_inc(self.sems[e], 1)
        ev = (e, self.cnt[e])
        self._mark(ev, reads, writes)
        self.ninst += 1
        return ev

    def dma(self, q, out, in_, **kw):
        reads = _bufs([in_])
        writes = _bufs([out])
        self._deps(q, reads, writes)
        key = self.dsem[q][self.dptr[q]]
        self.dptr[q] = (self.dptr[q] + 1) % len(self.dsem[q])
        if self.dcnt[key] > 0:
            self._wait(q, (key, self.dcnt[key]))
        inst = self.eng[q].dma_start(out=out.ap, in_=in_.ap, **kw)
        self.dcnt[key] += 16
        inst.then_inc(self.sems[key], 16)
        ev = (key, self.dcnt[key])
        self._mark(ev, reads, writes)
        self.ninst += 1
        return ev

    def barrier(self):
        evs = [(e, self.cnt[e]) for e in self.eng if self.cnt[e] > 0]
        evs += [(key, c) for key, c in self.dcnt.items() if c > 0]
        for e in self.eng:
            for ev in evs:
                if ev[0] != e:
                    self._wait(e, ev)

    def mm(self, out, lhsT, rhs, start=True, stop=True):
        return self.op("pe", lambda e: e.matmul(out.ap, lhsT.ap, rhs.ap, start=start, stop=stop),
                       reads=[lhsT, rhs], writes=[out], pe_accum=True)

    def tr(self, out, in_, ident):
        return self.op("pe", lambda e: e.transpose(out.ap, in_.ap, ident.ap),
                       reads=[in_, ident], writes=[out], pe_accum=True)

    def act(self, out, in_, func, scale=None, bias=None, accum=None):
        kw = {}
        rd = [in_]
        if scale is not None:
            kw["scale"] = scale.ap if isinstance(scale, V) else scale
            rd.append(scale)
        if bias is not None:
            kw["bias"] = bias.ap if isinstance(bias, V) else bias
            rd.append(bias)
        wr = [out]
        if accum is not None:
            kw["accum_out"] = accum.ap
            wr.append(accum)
        return self.op("act", lambda e: e.activation(out=out.ap, in_=in_.ap, func=func, **kw),
                       reads=rd, writes=wr)

    def tt(self, e, out, a, b, op):
        return self.op(e, lambda g: g.tensor_tensor(out=out.ap, in0=a.ap, in1=b.ap, op=op),
                       reads=[a, b], writes=[out])

    def ts(self, e, out, a, s1, op0, s2=None, op1=None, accum=None):
        kw = {}
        wr = [out]
        if op1 is not None:
            kw["op1"] = op1
        if accum is not None:
            kw["accum_out"] = accum.ap
            wr.append(accum)
        sa1 = s1.ap if isinstance(s1, V) else s1
        sa2 = s2.ap if isinstance(s2, V) else s2
        return self.op(e, lambda g: g.tensor_scalar(out=out.ap, in0=a.ap, scalar1=sa1, scalar2=sa2,
                                                    op0=op0, **kw),
                       reads=[a, s1, s2], writes=wr)

    def stt(self, out, in0, scalar, in1, op0, op1):
        sa = scalar.ap if isinstance(scalar, V) else scalar
        return self.op("dve", lambda g: g.scalar_tensor_tensor(out=out.ap, in0=in0.ap, scalar=sa,
                                                               in1=in1.ap, op0=op0, op1=op1),
                       reads=[in0, scalar, in1], writes=[out])

    def copy(self, e, out, in_):
        if e == "act":
            return self.op("act", lambda g: g.copy(out=out.ap, in_=in_.ap), reads=[in_], writes=[out])
        return self.op(e, lambda g: g.tensor_copy(out=out.ap, in_=in_.ap), reads=[in_], writes=[out])

    def memset(self, e, out, val):
        return self.op(e, lambda g: g.memset(out.ap, val), reads=[], writes=[out])

    def reduce(self, out, in_, op, axis=AX.X):
        return self.op("dve", lambda g: g.tensor_reduce(out=out.ap, in_=in_.ap, axis=axis, op=op),
                       reads=[in_], writes=[out])

    def recip(self, out, in_):
        return self.op("dve", lambda g: g.reciprocal(out=out.ap, in_=in_.ap), reads=[in_], writes=[out])


class Ring:
    def __init__(self, k, name, n, shape, dt, psum=False):
        self.bufs = [(k.ps if psum else k.sb)("%s%d" % (name, i), shape, dt) for i in range(n)]
        self.i = 0

    def next(self):
        b = self.bufs[self.i]
        self.i = (self.i + 1) % len(self.bufs)
        return b


class Prog:
    def __init__(self, S, kinds, wspecs):
        self.S = S
        self.kinds = kinds
        self.NT = S // 128
        self.NG = S // TG
        nc = bass.Bass("TRN2", target_bir_lowering=False)
        self.nc = nc
        self.k = K(nc)
        k = self.k
        self.ext = {}
        for name, shp in wspecs.items():
            self.ext[name] = Buf(name, nc.dram_tensor(name, list(shp), F32, kind="ExternalInput").ap())
        self.cext = {}
        for name, (shp, dt) in self.const_specs().items():
            self.cext[name] = Buf(name, nc.dram_tensor(name, list(shp), dt, kind="ExternalInput").ap())
        self.y = Buf("y", nc.dram_tensor("y", [S, D], F32, kind="ExternalOutput").ap())
        self.wb = {}

    def const_specs(self):
        S = self.S
        return {
            "c_identb": ([128, 128], BF16),
            "c_identf": ([128, 128], F32),
            "c_uincl": ([128, 128], BF16),
            "c_ones": ([128, 128], BF16),
            "c_sbmask": ([4, 128, 512], F32),
            "c_tribd": ([128, 128], F32),
            "c_onesbd": ([128, 128], F32),
            "c_id64": ([128, 64], F32),
            "c_mst": ([128, 64], F32),
            "c_mit": ([128, 64], F32),
            "c_ms": ([128, 64], F32),
            "c_cos": ([S, 32], F32),
            "c_sin": ([S, 32], F32),
        }

    @staticmethod
    def const_values(S):
        bf = ml_dtypes.bfloat16
        idx = np.arange(128)
        c = {}
        c["c_identb"] = np.eye(128, dtype=np.float32).astype(bf)
        c["c_identf"] = np.eye(128, dtype=np.float32)
        c["c_uincl"] = (idx[:, None] >= idx[None, :]).astype(np.float32).astype(bf)
        c["c_ones"] = np.ones((128, 128), np.float32).astype(bf)
        tq = np.arange(512)
        c["c_sbmask"] = np.stack([(tq[None, :] > (idx[:, None] + 128 * i)).astype(np.float32)
                                  for i in range(4)])
        same = (idx[:, None] // 64) == (idx[None, :] // 64)
        c["c_tribd"] = (same & (idx[:, None] <= idx[None, :])).astype(np.float32)
        c["c_onesbd"] = same.astype(np.float32)
        p64 = (idx % 64)[:, None]
        t64 = np.arange(64)[None, :]
        c["c_id64"] = (p64 == t64).astype(np.float32)
        c["c_mst"] = (t64 > p64).astype(np.float32)
        c["c_mit"] = (t64 >= p64).astype(np.float32)
        c["c_ms"] = (t64 < p64).astype(np.float32)
        half = 32
        inv = 1.0 / (10000.0 ** (np.arange(half, dtype=np.float32) / half))
        ang = np.arange(S, dtype=np.float32)[:, None] * inv[None, :]
        c["c_cos"] = np.cos(ang).astype(np.float32)
        c["c_sin"] = np.sin(ang).astype(np.float32)
        return c

    def cast_weight(self, key, src_ap, Kd, N):
        k = self.k
        dst = k.dram("wb_" + key, [Kd, N], BF16)
        src = Buf("src_" + key, src_ap)
        rows = 256
        for r0 in range(0, Kd, rows):
            r1 = min(Kd, r0 + rows)
            k.dma("pool", dst[r0:r1, :], src[r0:r1, :], max_dma_last_dim=4096)
        self.wb[key] = (dst, Kd, N)

    def wsrc(self, key, c0, cw):
        dst, Kd, N = self.wb[key]
        if Kd % 128 == 0:
            return dst.v(dst.ap.rearrange("(kc p) n -> p kc n", p=128)[:, :, c0:c0 + cw])
        return dst[:, c0:c0 + cw]

    def norm_transpose(self, xt, gam, hT, sub, pools, col0=0):
        k = self.k
        junk, ssb, hb, pT = pools["junk"].next(), pools["ss"].next(), pools["hb"].next(), pools["pT"].next()
        k.act(junk[:], xt, AF.Square, accum=ssb[:, 0:1])
        k.act(ssb[:, 1:2], ssb[:, 0:1], AF.Sqrt, scale=1.0 / D, bias=pools["eps"][:, 0:1])
        k.recip(ssb[:, 2:3], ssb[:, 1:2])
        k.stt(hb[:], xt, ssb[:, 2:3], gam, ALU.mult, ALU.mult)
        for kc in range(8):
            k.tr(pT[:, kc * 128:(kc + 1) * 128], hb[:, kc * 128:(kc + 1) * 128], pools["identb"][:])
        c = col0 + sub * 128
        k.copy("dve", hT[:, :, c:c + 128], pT.v(pT.ap.rearrange("p (k t) -> p k t", k=8)))

    def phase_c(self, li, o_buf, o_major, wout_key, xin):
        k = self.k
        S, NG = self.S, self.NG
        with ExitStack() as ph:
            k.stack = ph
            identb = k.sb("identb", [128, 128], BF16)
            k.dma("sp", identb[:], self.cext["c_identb"][:])
            eps = k.sb("eps", [128, 1], F32)
            k.memset("dve", eps[:], EPS)
            gam = k.sb("gam", [128, D], F32)
            k.dma("sp", gam[:], self.ext["norm_ffn"].v(self.ext["norm_ffn"].ap[li:li + 1, :].partition_broadcast(128)))
            pools = {"junk": Ring(k, "junk", 1, [128, D], F32), "ss": Ring(k, "ss", 2, [128, 4], F32),
                     "hb": Ring(k, "hb", 2, [128, D], BF16), "pT": Ring(k, "pT", 1, [128, D], BF16, psum=True),
                     "identb": identb, "eps": eps}
            oT = Ring(k, "oT", 1, [128, 8, TG], BF16)
            ot = Ring(k, "ot", 2, [128, D], BF16)
            wout = Ring(k, "wout", 2, [128, 8, 512], BF16)
            x1 = [k.sb("x1_%d" % i, [128, D], F32) for i in range(4)]
            xi = Ring(k, "xi", 2, [128, D], F32)
            h2T = Ring(k, "h2T", 1, [128, 8, TG], BF16)
            wgu = Ring(k, "wgu", 3, [128, 8, 512], BF16)
            aT = Ring(k, "aT", 1, [128, 22, TG], BF16)
            wd = Ring(k, "wd", 2, [128, 22, 512], BF16)
            sg = Ring(k, "sg", 2, [128, TG], F32)
            xo = Ring(k, "xo", 2, [128, D], F32)
            pacc = Ring(k, "pacc", 4, [128, 512], F32, psum=True)
            for g in range(NG):
                t0 = g * TG
                oTt = oT.next()
                if o_major == "f":
                    src = o_buf.rv(["g%d" % g], o_buf.ap.rearrange("(kc p) s -> p kc s", p=128)[:, :, t0:t0 + TG])
                    k.dma("sp", oTt[:], src)
                else:
                    for sub in range(4):
                        ott = ot.next()
                        r0 = t0 + sub * 128
                        k.dma("sp", ott[:], o_buf.rv(["g%d" % g], o_buf.ap[r0:r0 + 128, :]))
                        pT = pools["pT"].next()
                        for kc in range(8):
                            k.tr(pT[:, kc * 128:(kc + 1) * 128], ott[:, kc * 128:(kc + 1) * 128], identb[:])
                        k.copy("dve", oTt[:, :, sub * 128:(sub + 1) * 128],
                               pT.v(pT.ap.rearrange("p (k t) -> p k t", k=8)))
                wo = [wout.next(), wout.next()]
                for nb in range(2):
                    k.dma("sp", wo[nb][:], self.wsrc(wout_key, nb * 512, 512))
                h2 = h2T.next()
                for sub in range(4):
                    r0 = t0 + sub * 128
                    xt = xi.next()
                    k.dma("sp", xt[:], xin.rv(["t%d" % (r0 // 128)], xin.ap[r0:r0 + 128, :]))
                    for nb in range(2):
                        pa = pacc.next()
                        for kc in range(8):
                            k.mm(pa[:], oTt[:, kc, sub * 128:(sub + 1) * 128], wo[nb][:, kc, :],
                                 start=(kc == 0), stop=(kc == 7))
                        k.tt("dve", x1[sub][:, nb * 512:(nb + 1) * 512], pa[:], xt[:, nb * 512:(nb + 1) * 512], ALU.add)
                    self.norm_transpose(x1[sub][:], gam[:], h2, sub, pools)
                a = aT.next()
                for fb in range(11):
                    w = wgu.next()
                    k.dma("sp", w[:, :, 0:256], self.wsrc("gu%d" % li, fb * 256, 256))
                    k.dma("sp", w[:, :, 256:512], self.wsrc("gu%d" % li, FF + fb * 256, 256))
                    for jj in range(2):
                        j = 2 * fb + jj
                        pg = pacc.next()
                        pu = pacc.next()
                        for kc in range(8):
                            k.mm(pg[:], w[:, kc, jj * 128:(jj + 1) * 128], h2[:, kc, :], start=(kc == 0), stop=(kc == 7))
                        for kc in range(8):
                            k.mm(pu[:], w[:, kc, 256 + jj * 128:256 + (jj + 1) * 128], h2[:, kc, :],
                                 start=(kc == 0), stop=(kc == 7))
                        s = sg.next()
                        k.act(s[:], pg[:], AF.Silu)
                        k.tt("dve", a[:, j, :], s[:], pu[:], ALU.mult)
                wds = [wd.next(), wd.next()]
                for nb in range(2):
                    k.dma("sp", wds[nb][:], self.wsrc("dn%d" % li, nb * 512, 512))
                for sub in range(4):
                    r0 = t0 + sub * 128
                    xot = xo.next()
                    for nb in range(2):
                        pa = pacc.next()
                        for j in range(22):
                            k.mm(pa[:], a[:, j, sub * 128:(sub + 1) * 128], wds[nb][:, j, :],
                                 start=(j == 0), stop=(j == 21))
                        k.tt("dve", xot[:, nb * 512:(nb + 1) * 512], pa[:], x1[sub][:, nb * 512:(nb + 1) * 512], ALU.add)
                    k.dma("pool", self.y.rv(["t%d" % (r0 // 128)], self.y.ap[r0:r0 + 128, :]), xot[:])
            k.barrier()
        k.stack = k.base

    def front_pools(self, li, gam_name):
        k = self.k
        identb = k.sb("identb", [128, 128], BF16)
        k.dma("sp", identb[:], self.cext["c_identb"][:])
        eps = k.sb("eps", [128, 1], F32)
        k.memset("dve", eps[:], EPS)
        gam = k.sb("gam", [128, D], F32)
        src = self.ext[gam_name]
        k.dma("sp", gam[:], src.v(src.ap[li:li + 1, :].partition_broadcast(128)))
        pools = {"junk": Ring(k, "junk", 1, [128, D], F32), "ss": Ring(k, "ss", 2, [128, 4], F32),
                 "hb": Ring(k, "hb", 2, [128, D], BF16), "pT": Ring(k, "pT", 1, [128, D], BF16, psum=True),
                 "identb": identb, "eps": eps, "gam": gam}
        return pools

    def sb_phase_a(self, li, j, xin, qT, kT, vP):
        k = self.k
        S, NG = self.S, self.NG
        with ExitStack() as ph:
            k.stack = ph
            pools = self.front_pools(li, "norm_mix")
            xi = Ring(k, "xi", 2, [128, D], F32)
            hTr = Ring(k, "hT", 2, [128, 8, TG], BF16)
            wr = Ring(k, "w", 3, [128, 8, 512], BF16)
            ob = Ring(k, "ob", 3, [128, 512], BF16)
            pacc = Ring(k, "pacc", 4, [128, 512], F32, psum=True)
            cp = 0
            for g in range(NG):
                t0 = g * TG
                hT = hTr.next()
                for sub in range(4):
                    r0 = t0 + sub * 128
                    xt = xi.next()
                    k.dma("sp", xt[:], xin.rv(["t%d" % (r0 // 128)], xin.ap[r0:r0 + 128, :]))
                    self.norm_transpose(xt[:], pools["gam"][:], hT, sub, pools)
                for blk in range(6):
                    w = wr.next()
                    k.dma("sp", w[:], self.wsrc("sbqkv", blk * 512, 512))
                    if blk < 4:
                        dst = qT if blk < 2 else kT
                        for cc in range(4):
                            pa = pacc.next()
                            for kc in range(8):
                                k.mm(pa[:], w[:, kc, cc * 128:(cc + 1) * 128], hT[:, kc, :], start=(kc == 0), stop=(kc == 7))
                            o = ob.next()
                            k.copy("act" if cp % 2 else "dve", o[:], pa[:])
                            cp += 1
                            row = ((blk % 2) * 4 + cc) * 128
                            k.dma("pool", dst.rv(["g%d" % g], dst.ap[row:row + 128, t0:t0 + TG]), o[:])
                    else:
                        nb = blk - 4
                        for sub in range(4):
                            pa = pacc.next()
                            for kc in range(8):
                                k.mm(pa[:], hT[:, kc, sub * 128:(sub + 1) * 128], w[:, kc, :], start=(kc == 0), stop=(kc == 7))
                            o = ob.next()
                            k.copy("act" if cp % 2 else "dve", o[:], pa[:])
                            cp += 1
                            tix = g * 4 + sub
                            k.dma("pool", vP.rv(["g%d" % g], vP.ap[nb * 4:(nb + 1) * 4, :, tix, :].rearrange("h p c -> p h c")),
                                  o.v(o.ap.rearrange("p (h c) -> p h c", h=4)))
            k.barrier()
        k.stack = k.base

    def sb_phase_b(self, qT, kT, vP, oT, NHP=8):
        k = self.k
        S, NG, NT = self.S, self.NG, self.NT
        allg = ["g%d" % g for g in range(NG)]
        with ExitStack() as ph:
            k.stack = ph
            uincl = k.sb("uincl", [128, 128], BF16)
            ones = k.sb("ones", [128, 128], BF16)
            k.dma("sp", uincl[:], self.cext["c_uincl"][:])
            k.dma("sp", ones[:], self.cext["c_ones"][:])
            msk = k.sb("msk", [128, 4, 512], F32)
            k.dma("sp", msk[:], self.cext["c_sbmask"].v(self.cext["c_sbmask"].ap.rearrange("i s t -> s i t")))
            kTp = k.sb("kTp", [128, S], BF16)
            qTp = k.sb("qTp", [128, S], BF16)
            vp = k.sb("vp", [128, NT, 128], BF16)
            zr = Ring(k, "z", 3, [128, 512], F32, psum=True)
            cr = Ring(k, "c", 3, [128, 512], F32, psum=True)
            orr = Ring(k, "o", 2, [128, 512], F32, psum=True)
            er = Ring(k, "e", 5, [128, 512], F32)
            spr = Ring(k, "sp", 4, [128, 512], BF16)
            sumr = Ring(k, "sum", 4, [128, 512], BF16)
            ecr = Ring(k, "ec", 3, [128, 512], F32)
            atr = Ring(k, "att", 4, [128, 512], BF16)
            osb = Ring(k, "osb", 2, [64, 512], BF16)
            def s0(job):
                hp, e, qb, kt, first, last = job["key"]
                pb = 64 * e
                z = zr.next()
                k.mm(z[:], kTp[pb:pb + 64, kt * 128:(kt + 1) * 128], qTp[pb:pb + 64, qb * 512:(qb + 1) * 512])
                job["z"] = z

            def s1(job):
                hp, e, qb, kt, first, last = job["key"]
                et = er.next()
                k.act(et[:], job["z"][:], AF.Exp, scale=0.125)
                i = kt - 4 * qb
                if i >= 0:
                    k.tt("dve", et[:], et[:], msk[:, i, :], ALU.mult)
                spt = spr.next()
                k.act(spt[:], et[:], AF.Ln, bias=1.0)
                job["et"] = et
                job["spt"] = spt

            def s2(job):
                hp, e, qb, kt, first, last = job["key"]
                c = cr.next()
                spt = job["spt"]
                k.mm(c[:], uincl[:], spt[:], start=True, stop=first)
                if not first:
                    k.mm(c[:], ones[:], job["ssum_in"][:], start=False, stop=True)
                if first:
                    k.copy("pool", job["ssum_out"][:], spt[:])
                elif not last:
                    k.tt("pool", job["ssum_out"][:], job["ssum_in"][:], spt[:], ALU.add)
                job["c"] = c

            def s3(job):
                ec = ecr.next()
                k.act(ec[:], job["c"][:], AF.Exp, scale=-1.0)
                at = atr.next()
                k.tt("dve", at[:], job["et"][:], ec[:], ALU.mult)
                job["at"] = at

            def s4(job):
                hp, e, qb, kt, first, last = job["key"]
                pb = 64 * e
                oacc = job["oacc"]
                k.mm(oacc[0:64, :], vp[:, kt, pb:pb + 64], job["at"][:], start=first, stop=last)
                if last:
                    o = osb.next()
                    k.copy("dve", o[:], oacc[0:64, :])
                    row = hp * 128 + pb
                    k.dma("pool", oT.rv(["g%d" % qb], oT.ap[row:row + 64, qb * 512:(qb + 1) * 512]), o[:])

            stages = [s0, s1, s2, s3, s4]
            for hp in range(NHP):
                k.dma("sp", kTp[:], kT.rv(allg, kT.ap[hp * 128:(hp + 1) * 128, :]))
                k.dma("sp", qTp[:], qT.rv(allg, qT.ap[hp * 128:(hp + 1) * 128, :]))
                k.dma("sp", vp[:], vP.rv(allg, vP.ap[hp]))
                jobs = []
                for e in range(2):
                    for qb in range(NG):
                        oacc = orr.next()
                        kts = list(range(4 * qb + 3, -1, -1))
                        prev = None
                        for n, kt in enumerate(kts):
                            cur = sumr.next()
                            jobs.append({"key": (hp, e, qb, kt, n == 0, n == len(kts) - 1), "oacc": oacc,
                                         "ssum_in": prev, "ssum_out": cur})
                            prev = cur
                for n in range(len(jobs) + len(stages) - 1):
                    for si, fn in enumerate(stages):
                        if 0 <= n - si < len(jobs):
                            fn(jobs[n - si])
            k.barrier()
        k.stack = k.base

    def rope(self, src3, tabs, out3, nh, tmp):
        k = self.k
        A, Bm, Cc, Dd = [V(t.bufs, t.ap.unsqueeze(1).to_broadcast([128, nh, 32])) for t in tabs]
        x1 = V(src3.bufs, src3.ap[:, :, 0:32])
        x2 = V(src3.bufs, src3.ap[:, :, 32:64])
        t1, t2, t3, t4 = [t[:, 0:nh, :] for t in tmp]
        k.tt("dve", t1, x1, A, ALU.mult)
        k.tt("dve", t2, x2, Bm, ALU.mult)
        k.tt("dve", V(out3.bufs, out3.ap[:, :, 0:32]), t1, t2, ALU.subtract)
        k.tt("pool", t3, x1, Cc, ALU.mult)
        k.tt("pool", t4, x2, Dd, ALU.mult)
        k.tt("pool", V(out3.bufs, out3.ap[:, :, 32:64]), t3, t4, ALU.add)

    def ds_phase_a(self, li, j, xin, qT, kT, vP, qiT, kiT, wiS):
        k = self.k
        S, NG = self.S, self.NG
        with ExitStack() as ph:
            k.stack = ph
            pools = self.front_pools(li, "norm_mix")
            identb = pools["identb"]
            xi = Ring(k, "xi", 2, [128, D], F32)
            hTr = Ring(k, "hT", 1, [128, 8, TG], BF16)
            wr = Ring(k, "w", 3, [128, 8, 512], BF16)
            ob = Ring(k, "ob", 3, [128, 512], BF16)
            pacc = Ring(k, "pacc", 4, [128, 512], F32, psum=True)
            qk = [[k.sb("qk%d_%d" % (w_, s_), [128, D], F32) for s_ in range(4)] for w_ in range(2)]
            qis = [k.sb("qis%d" % s_, [128, 512], F32) for s_ in range(4)]
            kws = [k.sb("kws%d" % s_, [128, 72], F32) for s_ in range(4)]
            gqk = k.sb("gqk", [128, 2, 64], F32)
            for w_, nm in enumerate(("ds_q_norm", "ds_k_norm")):
                src = self.ext[nm]
                k.dma("sp", gqk[:, w_, :], src.v(src.ap[j:j + 1, :].partition_broadcast(128)))
            csr = Ring(k, "cs", 2, [128, 2, 32], F32)
            tabr = Ring(k, "tab", 2, [128, 2, 4, 32], F32)
            tmp = [k.sb("rtmp%d" % i, [128, 16, 32], F32) for i in range(4)]
            ssr = Ring(k, "hss", 2, [128, 3, 16], F32)
            rb = Ring(k, "rb", 2, [128, D], BF16)
            rbi = Ring(k, "rbi", 2, [128, 512 + 64], BF16)
            wir = Ring(k, "wir", 2, [128, 8], F32)
            qTg = [k.sb("qTg%d" % w_, [128, 8, TG], BF16) for w_ in range(2)]
            qiTg = k.sb("qiTg", [128, 4, TG], BF16)
            kiTg = k.sb("kiTg", [64, TG], BF16)
            junk = pools["junk"]
            cp = 0
            for g in range(NG):
                t0 = g * TG
                hT = hTr.next()
                for sub in range(4):
                    r0 = t0 + sub * 128
                    xt = xi.next()
                    k.dma("sp", xt[:], xin.rv(["t%d" % (r0 // 128)], xin.ap[r0:r0 + 128, :]))
                    self.norm_transpose(xt[:], pools["gam"][:], hT, sub, pools)
                for blk in range(8):
                    cw = 512 if blk < 7 else 72
                    w = wr.next()
                    k.dma("sp", w[:, :, 0:cw], self.wsrc("dsin", blk * 512, cw))
                    for sub in range(4):
                        pa = pacc.next()
                        for kc in range(8):
                            k.mm(pa[:, 0:cw], hT[:, kc, sub * 128:(sub + 1) * 128], w[:, kc, 0:cw], start=(kc == 0), stop=(kc == 7))
                        eng = "act" if cp % 2 else "dve"
                        cp += 1
                        if blk < 4:
                            k.copy(eng, qk[blk // 2][sub][:, (blk % 2) * 512:(blk % 2 + 1) * 512], pa[:])
                        elif blk < 6:
                            nb = blk - 4
                            o = ob.next()
                            k.copy(eng, o[:], pa[:])
                            tix = g * 4 + sub
                            k.dma("pool", vP.rv(["g%d" % g], vP.ap[nb * 4:(nb + 1) * 4, :, tix, :].rearrange("h p c -> p h c")),
                                  o.v(o.ap.rearrange("p (h c) -> p h c", h=4)))
                        elif blk == 6:
                            k.copy(eng, qis[sub][:], pa[:])
                        else:
                            k.copy(eng, kws[sub][:], pa[:, 0:72])
                for sub in range(4):
                    r0 = t0 + sub * 128
                    cs = csr.next()
                    k.dma("sp", cs[:, 0, :], self.cext["c_cos"][r0:r0 + 128, :])
                    k.dma("sp", cs[:, 1, :], self.cext["c_sin"][r0:r0 + 128, :])
                    tab = tabr.next()
                    for w_ in range(2):
                        k.tt("dve", tab[:, w_, 0, :], cs[:, 0, :], gqk[:, w_, 0:32], ALU.mult)
                        k.tt("dve", tab[:, w_, 1, :], cs[:, 1, :], gqk[:, w_, 32:64], ALU.mult)
                        k.tt("dve", tab[:, w_, 2, :], cs[:, 1, :], gqk[:, w_, 0:32], ALU.mult)
                        k.tt("dve", tab[:, w_, 3, :], cs[:, 0, :], gqk[:, w_, 32:64], ALU.mult)
                    for w_ in range(2):
                        src = qk[w_][sub]
                        jk = junk.next()
                        ss = ssr.next()
                        k.act(jk[:], src[:], AF.Square)
                        k.reduce(ss[:, 0, :], jk.v(jk.ap.rearrange("p (h d) -> p h d", h=16)), ALU.add)
                        k.act(ss[:, 1, :], ss[:, 0, :], AF.Sqrt, scale=1.0 / 64, bias=pools["eps"][:, 0:1])
                        k.recip(ss[:, 2, :], ss[:, 1, :])
                        s3 = src.v(src.ap.rearrange("p (h d) -> p h d", h=16))
                        k.tt("dve", s3, s3, V(ss.v(ss.ap).bufs, ss.ap[:, 2, :].unsqueeze(2).to_broadcast([128, 16, 64])), ALU.mult)
                        r = rb.next()
                        self.rope(s3, [tab[:, w_, i, :] for i in range(4)], r.v(r.ap.rearrange("p (h d) -> p h d", h=16)), 16, tmp)
                        pT = pools["pT"].next()
                        for kc in range(8):
                            k.tr(pT[:, kc * 128:(kc + 1) * 128], r[:, kc * 128:(kc + 1) * 128], identb[:])
                        k.copy("act", qTg[w_][:, :, sub * 128:(sub + 1) * 128], pT.v(pT.ap.rearrange("p (k t) -> p k t", k=8)))
                    ri = rbi.next()
                    plain = [cs[:, 0, :], cs[:, 1, :], cs[:, 1, :], cs[:, 0, :]]
                    q3 = qis[sub].v(qis[sub].ap.rearrange("p (h d) -> p h d", h=8))
                    self.rope(q3, plain, ri.v(ri.ap[:, 0:512].rearrange("p (h d) -> p h d", h=8)), 8, tmp)
                    k3 = kws[sub].v(kws[sub].ap[:, 0:64].rearrange("p (h d) -> p h d", h=1))
                    self.rope(k3, plain, ri.v(ri.ap[:, 512:576].rearrange("p (h d) -> p h d", h=1)), 1, tmp)
                    pT = pools["pT"].next()
                    for kc in range(4):
                        k.tr(pT[:, kc * 128:(kc + 1) * 128], ri[:, kc * 128:(kc + 1) * 128], identb[:])
                    k.tr(pT[0:64, 512:640], ri[:, 512:576], identb[:])
                    k.copy("act", qiTg[:, :, sub * 128:(sub + 1) * 128], pT.v(pT.ap[:, 0:512].rearrange("p (k t) -> p k t", k=4)))
                    k.copy("act", kiTg[:, sub * 128:(sub + 1) * 128], pT[0:64, 512:640])
                    wi = wir.next()
                    k.ts("dve", wi[:], kws[sub][:, 64:72], float(8 ** -0.5 * 64 ** -0.5), ALU.mult)
                    k.dma("pool", wiS.rv(["g%d" % g], wiS.ap[r0:r0 + 128, :]), wi[:])
                for w_, dst in enumerate((qT, kT)):
                    k.dma("pool", dst.rv(["g%d" % g], dst.ap.rearrange("(kc p) s -> p kc s", p=128)[:, :, t0:t0 + TG]), qTg[w_][:])
                k.dma("pool", qiT.rv(["g%d" % g], qiT.ap.rearrange("(kc p) s -> p kc s", p=128)[:, :, t0:t0 + TG]), qiTg[:])
                k.dma("pool", kiT.rv(["g%d" % g], kiT.ap[:, t0:t0 + TG]), kiTg[:])
            k.barrier()
        k.stack = k.base

    def ds_phase_b1(self, qiT, kiT, wiS, maskT):
        k = self.k
        S, NG, NT = self.S, self.NG, self.NT
        allg = ["g%d" % g for g in range(NG)]
        NIT = 17
        with ExitStack() as ph:
            k.stack = ph
            identb = k.sb("identb", [128, 128], BF16)
            k.dma("sp", identb[:], self.cext["c_identb"][:])
            kiT2 = k.sb("kiT2", [128, S], BF16)
            k.dma("sp", kiT2[0:64, :], kiT.rv(allg, kiT.ap))
            k.dma("sp", kiT2[64:128, :], kiT.rv(allg, kiT.ap))
            scores = [k.sb("score%d" % i, [128, S], F32) for i in range(2)]
            junks = [k.sb("junkb%d" % i, [128, S], BF16) for i in range(2)]
            mk = k.sb("mk", [128, S], BF16)
            qibr = Ring(k, "qib", 2, [128, 4, 128], BF16)
            wibr = Ring(k, "wib", 2, [128, 8], F32)
            rlr = Ring(k, "rl", 4, [128, 512], F32)
            pidx = Ring(k, "pidx", 4, [128, 512], F32, psum=True)
            pTm = Ring(k, "pTm", 2, [128, 1024], BF16, psum=True)
            mTr = Ring(k, "mT", 2, [128, NT, 128], BF16)
            sts = [k.sb("st%d" % i, [128, 8], F32) for i in range(2)]

            def compute_scores(qb, score):
                N = 128 * (qb + 1)
                qib = qibr.next()
                wib = wibr.next()
                k.dma("sp", qib[:], qiT.rv(["g%d" % (qb // 4)], qiT.ap.rearrange("(hp p) s -> p hp s", p=128)[:, :, qb * 128:(qb + 1) * 128]))
                k.dma("sp", wib[:], wiS.rv(["g%d" % (qb // 4)], wiS.ap[qb * 128:(qb + 1) * 128, :]))
                for c0 in range(0, N, 512):
                    cw = min(512, N - c0)
                    for h in range(8):
                        hp, pb = h // 2, 64 * (h % 2)
                        pi = pidx.next()
                        k.mm(pi[:, 0:cw], qib[pb:pb + 64, hp, :], kiT2[pb:pb + 64, c0:c0 + cw])
                        if h == 0:
                            k.ts("dve", score[:, c0:c0 + cw], pi[:, 0:cw], 0.0, ALU.max, wib[:, 0:1], ALU.mult)
                        elif h % 2 == 1:
                            r = rlr.next()
                            k.act(r[:, 0:cw], pi[:, 0:cw], AF.Relu)
                            k.stt(score[:, c0:c0 + cw], r[:, 0:cw], wib[:, h:h + 1], score[:, c0:c0 + cw], ALU.mult, ALU.add)
                        else:
                            r = rlr.next()
                            k.ts("dve", r[:, 0:cw], pi[:, 0:cw], 0.0, ALU.max, wib[:, h:h + 1], ALU.mult)
                            k.tt("pool", score[:, c0:c0 + cw], score[:, c0:c0 + cw], r[:, 0:cw], ALU.add)

            for q0 in range(0, NT, 2):
                qbs = [q for q in (q0, q0 + 1) if q < NT]
                for i, qb in enumerate(qbs):
                    N = 128 * (qb + 1)
                    st = sts[i]
                    compute_scores(qb, scores[i])
                    k.reduce(st[:, 1:2], scores[i][:, 0:N], ALU.max)
                    k.reduce(st[:, 0:1], scores[i][:, 0:N], ALU.min)
                    k.memset("dve", scores[i][0:64, N - 64:N], NEG)
                for it in range(NIT):
                    for i, qb in enumerate(qbs):
                        N = 128 * (qb + 1)
                        st = sts[i]
                        lo, hi, nmid, ssum, pred, d1, d2 = [st[:, j:j + 1] for j in range(7)]
                        k.ts("dve", nmid, lo, hi, ALU.add, -0.5, ALU.mult)
                        k.act(junks[i][:, 0:N], scores[i][:, 0:N], AF.Sign, bias=nmid, accum=ssum)
                    for i, qb in enumerate(qbs):
                        N = 128 * (qb + 1)
                        st = sts[i]
                        lo, hi, nmid, ssum, pred, d1, d2 = [st[:, j:j + 1] for j in range(7)]
                        k.ts("dve", pred, ssum, float(511 - N), ALU.is_ge)
                        k.stt(d1, nmid, -1.0, lo, ALU.mult, ALU.subtract)
                        k.tt("dve", d2, hi, nmid, ALU.add)
                        k.stt(lo, d1, pred, lo, ALU.mult, ALU.add)
                        k.stt(hi, d2, pred, nmid, ALU.mult, ALU.subtract)
                for i, qb in enumerate(qbs):
                    N = 128 * (qb + 1)
                    k.ts("dve", mk[:, 0:N], scores[i][:, 0:N], sts[i][:, 0:1], ALU.is_ge)
                    mT = mTr.next()
                    for s0 in range(0, qb + 1, 8):
                        n = min(8, qb + 1 - s0)
                        pt = pTm.next()
                        for ii in range(n):
                            k.tr(pt[:, ii * 128:(ii + 1) * 128], mk[:, (s0 + ii) * 128:(s0 + ii + 1) * 128], identb[:])
                        k.copy("dve", mT.v(mT.ap[:, s0:s0 + n, :].rearrange("p a b -> p (a b)")), pt[:, 0:n * 128])
                    off = qb * (qb + 1) // 2
                    k.dma("pool", maskT.rv(["q%d" % qb], maskT.ap[:, off:off + qb + 1, :]), mT[:, 0:qb + 1, :])
            k.barrier()
        k.stack = k.base

    def ds_phase_b2(self, qT, kT, vP, maskT, o_d, NHP=8):
        k = self.k
        S, NG, NT = self.S, self.NG, self.NT
        allg = ["g%d" % g for g in range(NG)]
        with ExitStack() as ph:
            k.stack = ph
            kTp = k.sb("kTp", [128, S], BF16)
            qTp = k.sb("qTp", [128, S], BF16)
            vp = k.sb("vp", [128, NT, 128], BF16)
            vaug = k.sb("vaug", [128, NT, 2, 65], BF16)
            mring = Ring(k, "m", 3, [128, NT, 128], BF16)
            lgr = Ring(k, "lg", 3, [128, 512], F32, psum=True)
            oar = Ring(k, "oa", 3, [128, 512], F32, psum=True)
            prr = Ring(k, "p", 5, [128, 512], BF16)
            pmr = Ring(k, "pm", 3, [128, 512], BF16)
            rdr = Ring(k, "rd", 2, [128, 1], F32)
            otr = Ring(k, "ot", 3, [128, 128], BF16)
            cp = [0]
            LAG = 2

            def stage1(job):
                qb, e, c0, n = job["key"]
                pb = 64 * e
                lg = lgr.next()
                for i in range(n):
                    k.mm(lg[:, i * 128:(i + 1) * 128], kTp[pb:pb + 64, (c0 + i) * 128:(c0 + i + 1) * 128],
                         qTp[pb:pb + 64, qb * 128:(qb + 1) * 128])
                p = prr.next()
                k.act(p[:, 0:n * 128], lg[:, 0:n * 128], AF.Exp, scale=0.125)
                job["p"] = p

            def stage2(job):
                qb, e, c0, n = job["key"]
                nst = qb + 1
                m, oa, ot, hp = job["m"], job["oa"], job["ot"], job["hp"]
                pm = pmr.next()
                k.tt("dve" if cp[0] % 2 else "pool", pm[:, 0:n * 128], job["p"][:, 0:n * 128],
                     m.v(m.ap[:, c0:c0 + n, :].rearrange("p a b -> p (a b)")), ALU.mult)
                cp[0] += 1
                for i in range(n):
                    k.mm(oa[:, 0:65], pm[:, i * 128:(i + 1) * 128], vaug[:, c0 + i, e, :],
                         start=(c0 + i == 0), stop=(c0 + i == nst - 1))
                if c0 + n == nst:
                    rd = rdr.next()
                    k.recip(rd[:], oa[:, 64:65])
                    k.ts("dve", ot[:, e * 64:(e + 1) * 64], oa[:, 0:64], rd[:, 0:1], ALU.mult)
                    if e == 1:
                        k.dma("pool", o_d.rv(["g%d" % (qb // 4)], o_d.ap[qb * 128:(qb + 1) * 128, hp * 128:(hp + 1) * 128]), ot[:])

            for hp in range(NHP):
                k.dma("sp", kTp[:], kT.rv(allg, kT.ap[hp * 128:(hp + 1) * 128, :]))
                k.dma("sp", qTp[:], qT.rv(allg, qT.ap[hp * 128:(hp + 1) * 128, :]))
                k.dma("sp", vp[:], vP.rv(allg, vP.ap[hp]))
                k.memset("pool", vaug[:], 1.0)
                k.copy("dve", vaug[:, :, :, 0:64], vp.v(vp.ap.rearrange("p t (e d) -> p t e d", e=2)))
                jobs = []
                for qb in range(NT):
                    nst = qb + 1
                    m = mring.next()
                    off = qb * (qb + 1) // 2
                    ot = otr.next()
                    for e in range(2):
                        oa = oar.next()
                        for c0 in range(0, nst, 4):
                            jobs.append({"key": (qb, e, c0, min(4, nst - c0)), "m": m, "oa": oa, "ot": ot, "hp": hp,
                                         "load": (e == 0 and c0 == 0), "off": off})
                for n in range(len(jobs) + LAG):
                    if n < len(jobs):
                        jb = jobs[n]
                        if jb["load"]:
                            qb = jb["key"][0]
                            k.dma("sp", jb["m"][:, 0:qb + 1, :], maskT.rv(["q%d" % qb], maskT.ap[:, jb["off"]:jb["off"] + qb + 1, :]))
                        stage1(jb)
                    if n >= LAG:
                        stage2(jobs[n - LAG])
            k.barrier()
        k.stack = k.base

    def rw_phase_a1a(self, li, j, xin, raw):
        k = self.k
        S, NG = self.S, self.NG
        with ExitStack() as ph:
            k.stack = ph
            pools = self.front_pools(li, "norm_mix")
            identf = k.sb("identf", [128, 128], F32)
            k.dma("sp", identf[:], self.cext["c_identf"][:])
            mixrow = k.sb("mixrow", [48, 128], F32)
            src = self.ext["rw_mix"]
            k.dma("sp", mixrow[:], src.v(src.ap[j].rearrange("s (kc p) -> (s kc) p", p=128)))
            pacc = Ring(k, "pacc", 4, [128, 512], F32, psum=True)
            mixT = k.sb("mixT", [128, 48], F32)
            pm0 = pacc.next()
            k.tr(pm0[:, 0:48], mixrow[:], identf[0:48, 0:48])
            k.copy("dve", mixT[:], pm0[:, 0:48])
            lor = {}
            specs = [("w", 64, AF.Tanh), ("a", 64, AF.Copy), ("g", 160, AF.Sigmoid)]
            if j >= 1:
                specs.append(("v", 32, AF.Copy))
            for nm, r, fn in specs:
                jj = j - 1 if nm == "v" else j
                w1 = k.sb("l1" + nm, [128, 8, r], BF16)
                k.dma("sp", w1[:], self.wsrc("rw%s1_%d" % (nm, jj), 0, r))
                chunks = []
                dstb = self.wb["rw%s2_%d" % (nm, jj)][0]
                for r0 in range(0, r, 128):
                    rn = min(128, r - r0)
                    w2 = k.sb("l2%s%d" % (nm, r0), [rn, D], BF16)
                    k.dma("sp", w2[:], dstb[r0:r0 + rn, :])
                    t1 = k.sb("t1%s%d" % (nm, r0), [rn, TG], BF16)
                    chunks.append((r0, rn, w2, t1))
                lor[nm] = (w1, chunks, fn)
            xi = Ring(k, "xi", 2, [128, D], F32)
            hTr = Ring(k, "hT", 2, [128, 8, TG + 1], BF16)
            dT = k.sb("dT", [128, 8, TG], BF16)
            xmr = Ring(k, "xm", 2, [128, 8, TG], BF16)
            wr = Ring(k, "w", 3, [128, 8, 512], BF16)
            otr = Ring(k, "ot", 4, [128, 4, 512], BF16)
            cp = [0]

            cur = {}

            def emit(pa, name, g, sub, nb):
                if sub == 0:
                    cur["o"] = otr.next()
                o = cur["o"]
                k.copy("act" if cp[0] % 2 else "dve", o[:, sub, :], pa[:])
                cp[0] += 1
                if sub == 3:
                    t0_ = g * TG
                    dst = raw[name]
                    keys = ["t%d" % (t0_ // 128 + i) for i in range(4)]
                    k.dma("pool", dst.rv(keys, dst.ap[t0_:t0_ + TG, nb * 512:(nb + 1) * 512].rearrange("(s p) c -> p s c", p=128)), o[:])

            def lora(nm, xm, g):
                w1, chunks, fn = lor[nm]
                for (r0, rn, w2, t1) in chunks:
                    p1 = pacc.next()
                    for kc in range(8):
                        k.mm(p1[0:rn, :], w1[:, kc, r0:r0 + rn], xm[:, kc, :], start=(kc == 0), stop=(kc == 7))
                    k.act(t1[:], p1[0:rn, :], fn)
                for nb in range(2):
                    for sub in range(4):
                        pa = pacc.next()
                        for ci, (r0, rn, w2, t1) in enumerate(chunks):
                            k.mm(pa[:], t1[0:rn, sub * 128:(sub + 1) * 128], w2[0:rn, nb * 512:(nb + 1) * 512],
                                 start=(ci == 0), stop=(ci == len(chunks) - 1))
                        emit(pa, "l" + nm, g, sub, nb)

            hprev = None
            for g in range(NG):
                t0 = g * TG
                hT = hTr.next()
                if hprev is None:
                    k.memset("pool", hT[:, :, 0:1], 0.0)
                else:
                    k.copy("pool", hT[:, :, 0:1], hprev[:, :, TG:TG + 1])
                for sub in range(4):
                    r0 = t0 + sub * 128
                    xt = xi.next()
                    k.dma("sp", xt[:], xin.rv(["t%d" % (r0 // 128)], xin.ap[r0:r0 + 128, :]))
                    self.norm_transpose(xt[:], pools["gam"][:], hT, sub, pools, col0=1)
                hprev = hT
                k.tt("dve", dT[:], hT[:, :, 0:TG], hT[:, :, 1:TG + 1], ALU.subtract)
                for s, name in enumerate(("r", "k", "v", "w", "a", "g")):
                    xm = xmr.next()
                    for kc in range(8):
                        k.stt(xm[:, kc, :], dT[:, kc, :], mixT[:, s * 8 + kc:s * 8 + kc + 1], hT[:, kc, 1:TG + 1], ALU.mult, ALU.add)
                    if s < 3:
                        for nb in range(2):
                            w = wr.next()
                            k.dma("sp", w[:], self.wsrc("rwrkv%d_%d" % (j, s), nb * 512, 512))
                            for sub in range(4):
                                pa = pacc.next()
                                for kc in range(8):
                                    k.mm(pa[:], xm[:, kc, sub * 128:(sub + 1) * 128], w[:, kc, :], start=(kc == 0), stop=(kc == 7))
                                emit(pa, name, g, sub, nb)
                        if s == 2 and j >= 1:
                            lora("v", xm, g)
                    else:
                        lora(name, xm, g)
            k.barrier()
        k.stack = k.base

    def rw_phase_a1b(self, li, j, raw, tok, feat, glt_d, vfirst):
        k = self.k
        S, NT = self.S, self.NT
        NC = S // 64
        with ExitStack() as ph:
            k.stack = ph
            identf = k.sb("identf", [128, 128], F32)
            k.dma("sp", identf[:], self.cext["c_identf"][:])
            tribd = k.sb("tribd", [128, 128], F32)
            onesbd = k.sb("onesbd", [128, 128], F32)
            k.dma("sp", tribd[:], self.cext["c_tribd"][:])
            k.dma("sp", onesbd[:], self.cext["c_onesbd"][:])
            prm = {}
            plist = [("w0", "rw_w0", j), ("a0", "rw_a0", j), ("kk", "rw_k_k", j), ("ka", "rw_k_a", j)]
            if j >= 1:
                plist.append(("v0", "rw_v0", j - 1))
            for nm, ext, jj in plist:
                t = k.sb("prm_" + nm, [128, D], F32)
                src = self.ext[ext]
                k.dma("sp", t[:], src.v(src.ap[jj:jj + 1, :].partition_broadcast(128)))
                prm[nm] = t
            t = k.sb("prm_rk", [128, D], F32)
            src = self.ext["rw_r_k"]
            k.dma("sp", t[:], src.v(src.ap[j].rearrange("h d -> (h d)").unsqueeze(0).partition_broadcast(128)))
            prm["rk"] = t
            glt = k.sb("glt", [128, 8, NC], F32)
            names_in = ["r", "k", "v", "lw", "la"] + (["lv"] if j >= 1 else [])
            inr = {n: Ring(k, "in_" + n, 3, [128, 512], BF16) for n in names_in}
            if j >= 1:
                inr["vf"] = Ring(k, "in_vf", 3, [128, 512], BF16)
            T = {n: Ring(k, "t_" + n, 2, [128, 512], F32) for n in
                 ["vw", "a", "ld", "kk", "kp", "b", "t1", "t2", "cs", "g", "gi", "gp", "gr", "gl",
                  "R", "P", "B", "K", "Bb", "Kb", "BV"]}
            sm = Ring(k, "sm", 2, [128, 4, 8], F32)
            ps_c = Ring(k, "psc", 2, [128, 512], F32, psum=True)
            ps_t = Ring(k, "pst", 2, [128, 512], F32, psum=True)
            ps_x = Ring(k, "psx", 3, [128, 512], F32, psum=True)
            xtr = Ring(k, "xT", 3, [128, 4, 128], F32)
            c24 = k.sb("c24", [128, 1], F32)
            k.memset("dve", c24[:], 0.0)
            def load_tiles(st, nb):
                r0 = st * 128
                cs_ = slice(nb * 512, (nb + 1) * 512)
                L = {}
                for n in names_in:
                    t = inr[n].next()
                    k.dma("sp", t[:], raw[n].rv(["t%d" % st], raw[n].ap[r0:r0 + 128, cs_]))
                    L[n] = t
                if j >= 1:
                    t = inr["vf"].next()
                    k.dma("sp", t[:], vfirst.rv(["t%d" % st], vfirst.ap[r0:r0 + 128, cs_]))
                    L["vf"] = t
                return L

            order = [(st, nb) for st in range(NT) for nb in range(2)]
            pending = load_tiles(*order[0])
            for oi, (st, nb) in enumerate(order):
                if True:
                    r0 = st * 128
                    cs_ = slice(nb * 512, (nb + 1) * 512)
                    L = pending
                    if oi + 1 < len(order):
                        pending = load_tiles(*order[oi + 1])
                    a, ld, kk, kp, b, t1, t2 = [T[n].next() for n in ("a", "ld", "kk", "kp", "b", "t1", "t2")]
                    s4 = sm.next()
                    k.tt("dve", a[:], L["la"][:], prm["a0"][:, cs_], ALU.add)
                    k.act(a[:], a[:], AF.Sigmoid)
                    k.tt("pool", ld[:], L["lw"][:], prm["w0"][:, cs_], ALU.add)
                    k.act(ld[:], ld[:], AF.Sigmoid)
                    k.ts("pool", ld[:], ld[:], -0.6065306597126334, ALU.mult)
                    v = T["vw"].next()
                    k.copy("pool", v[:], L["v"][:])
                    if j >= 1:
                        k.tt("dve", t1[:], L["lv"][:], prm["v0"][:, cs_], ALU.add)
                        k.act(t1[:], t1[:], AF.Sigmoid)
                        k.tt("dve", t2[:], L["vf"][:], v[:], ALU.subtract)
                        k.tt("dve", t2[:], t2[:], t1[:], ALU.mult)
                        k.tt("dve", v[:], v[:], t2[:], ALU.add)
                    else:
                        k.dma("sp", vfirst.rv(["t%d" % st], vfirst.ap[r0:r0 + 128, cs_]), L["v"][:])
                    k.dma("sp", tok["V"].rv(["t%d" % st], tok["V"].ap[r0:r0 + 128, cs_]), v[:])
                    k.tt("dve", kk[:], L["k"][:], prm["kk"][:, cs_], ALU.mult)
                    k.act(t1[:], kk[:], AF.Square)
                    k.reduce(s4[:, 0, :], t1.v(t1.ap.rearrange("p (h d) -> p h d", h=8)), ALU.add)
                    k.ts("dve", s4[:, 0, :], s4[:, 0, :], 1e-24, ALU.max)
                    k.act(s4[:, 1, :], s4[:, 0, :], AF.Sqrt)
                    k.recip(s4[:, 2, :], s4[:, 1, :])
                    kk3 = kk.v(kk.ap.rearrange("p (h d) -> p h d", h=8))
                    k.tt("dve", kk3, kk3, V(s4.v(s4.ap).bufs, s4.ap[:, 2, :].unsqueeze(2).to_broadcast([128, 8, 64])), ALU.mult)
                    k.stt(t2[:], a[:], -1.0, prm["ka"][:, cs_], ALU.add, ALU.mult)
                    k.stt(kp[:], t2[:], 1.0, L["k"][:], ALU.add, ALU.mult)
                    k.tt("pool", b[:], kk[:], a[:], ALU.mult)
                    k.tt("dve", t1[:], L["r"][:], kp[:], ALU.mult)
                    k.tt("dve", t1[:], t1[:], prm["rk"][:, cs_], ALU.mult)
                    k.reduce(s4[:, 3, :], t1.v(t1.ap.rearrange("p (h d) -> p h d", h=8)), ALU.add)
                    bv = T["BV"].next()
                    k.tt("dve", bv.v(bv.ap.rearrange("p (h d) -> p h d", h=8)), v.v(v.ap.rearrange("p (h d) -> p h d", h=8)),
                         V(s4.v(s4.ap).bufs, s4.ap[:, 3, :].unsqueeze(2).to_broadcast([128, 8, 64])), ALU.mult)
                    k.dma("sp", tok["BV"].rv(["t%d" % st], tok["BV"].ap[r0:r0 + 128, cs_]), bv[:])
                    pc = ps_c.next()
                    pt = ps_t.next()
                    k.mm(pc[:], tribd[:], ld[:])
                    k.mm(pt[:], onesbd[:], ld[:])
                    cs, g_, gi, gp, gr, gl = [T[n].next() for n in ("cs", "g", "gi", "gp", "gr", "gl")]
                    k.copy("dve", cs[:], pc[:])
                    k.act(g_[:], pc[:], AF.Exp)
                    k.act(gi[:], pc[:], AF.Exp, scale=-1.0)
                    k.tt("dve", gp[:], cs[:], ld[:], ALU.subtract)
                    k.act(gp[:], gp[:], AF.Exp)
                    k.tt("dve", gr[:], pt[:], cs[:], ALU.subtract)
                    k.act(gr[:], gr[:], AF.Exp)
                    k.act(gl[:], pt[:], AF.Exp)
                    R, P, B, Kt, Bb, Kb = [T[n].next() for n in ("R", "P", "B", "K", "Bb", "Kb")]
                    k.tt("dve", R[:], L["r"][:], g_[:], ALU.mult)
                    k.stt(P[:], kk[:], -1.0, gp[:], ALU.mult, ALU.mult)
                    k.tt("pool", B[:], b[:], gi[:], ALU.mult)
                    k.tt("pool", Kt[:], kp[:], gi[:], ALU.mult)
                    k.tt("pool", Bb[:], b[:], gr[:], ALU.mult)
                    k.tt("dve", Kb[:], kp[:], gr[:], ALU.mult)
                    k.dma("sp", tok["P"].rv(["t%d" % st], tok["P"].ap[r0:r0 + 128, cs_]), P[:])
                    k.dma("sp", tok["Bb"].rv(["t%d" % st], tok["Bb"].ap[r0:r0 + 128, cs_]), Bb[:])
                    k.dma("sp", tok["Kb"].rv(["t%d" % st], tok["Kb"].ap[r0:r0 + 128, cs_]), Kb[:])
                    for nm, src_t in (("P", P), ("R", R), ("B", B), ("K", Kt)):
                        px = ps_x.next()
                        for pr in range(4):
                            k.tr(px[:, pr * 128:(pr + 1) * 128], src_t[:, pr * 128:(pr + 1) * 128], identf[:])
                        xt = xtr.next()
                        k.copy("act", xt[:], px.v(px.ap.rearrange("p (a t) -> p a t", a=4)))
                        dst = feat[nm]
                        k.dma("sp", dst.rv(["t%d" % st], dst.ap.rearrange("(a p) s -> p a s", p=128)[:, nb * 4:(nb + 1) * 4, r0:r0 + 128]), xt[:])
                    px = ps_x.next()
                    for pr in range(4):
                        k.tr(px[:, pr * 128:(pr + 1) * 128], gl[:, pr * 128:(pr + 1) * 128], identf[:])
                    k.copy("act", glt[:, nb * 4:(nb + 1) * 4, 2 * st:2 * st + 2],
                           px.v(px.ap.rearrange("p (a c t) -> p a c t", a=4, c=2)[:, :, :, 0]))
            k.dma("pool", glt_d[:], glt[:])
            k.barrier()
        k.stack = k.base

    def rw_phase_b(self, tok, feat, glt_d, y_d):
        k = self.k
        S, NT = self.S, self.NT
        NC = S // 64
        with ExitStack() as ph:
            k.stack = ph
            cm = {}
            for nm in ("c_id64", "c_mst", "c_mit", "c_ms"):
                t = k.sb(nm, [128, 64], F32)
                k.dma("sp", t[:], self.cext[nm][:])
                cm[nm] = t
            glt = k.sb("glt", [128, 8, NC], F32)
            k.dma("sp", glt[:], glt_d[:])
            tkr = {n: Ring(k, "tk_" + n, 2, [128, D], F32) for n in ("V", "P", "Bb", "Kb")}
            ftr = {n: Ring(k, "ft_" + n, 2, [128, 8, 128], F32) for n in ("P", "R", "B", "K")}
            tkb = {n: k.sb("tkb_" + n, [128, D], BF16) for n in ("V", "P", "Bb", "Kb")}
            AT = {n: k.sb("AT_" + n, [128, 16, 64], (BF16 if n in ("rb", "pk", "rk") else F32)) for n in ("pb", "rb", "pk", "rk", "s0")}
            SN = {n: Ring(k, "nm_" + n, 2, [128, 8, 64], F32) for n in ("S", "N")}
            QT = k.sb("QT", [128, 16, 64], BF16)
            Qr = Ring(k, "Qr", 2, [128, 8, 64], F32)
            W1 = k.sb("W1", [128, 16, 64], BF16)
            U0 = k.sb("U0", [128, 16, 64], BF16)
            PH = k.sb("PH", [128, 16, 64], BF16)
            DG = k.sb("DG", [128, 2, 8, 64], F32)
            MT = k.sb("MT", [128, 2, 8, 64], F32)
            NCs = k.sb("NCs", [128, 2, 8, 64], F32)
            RHT = k.sb("RHT", [128, 2, 8, 64], BF16)
            Hbr = Ring(k, "Hb", 2, [128, 8, 64], BF16)
            Y0 = k.sb("Y0", [128, 16, 64], F32)
            Hr = Ring(k, "H", 2, [128, 8, 64], F32)
            ysb = Ring(k, "ysb", 2, [128, D], F32)
            PP = [k.ps("PP%d" % i, [128, 1024], F32) for i in range(4)]

            def bc(buf, n):
                return V((buf,), buf.ap.unsqueeze(1).to_broadcast([128, n, 64]))

            def v16(pp):
                return pp.v(pp.ap.rearrange("p (h t) -> p h t", h=16))

            def v28(pp):
                return pp.v(pp.ap.rearrange("p (c a t) -> p c a t", c=2, a=8))

            HE = [(hp, e, 2 * hp + e, e * 8 + hp) for e in range(2) for hp in range(8)]
            def load_st(st):
                r0 = st * 128
                tk_, ft_ = {}, {}
                for n in tkr:
                    tk_[n] = tkr[n].next()
                    k.dma("sp", tk_[n][:], tok[n].rv(["t%d" % st], tok[n].ap[r0:r0 + 128, :]))
                for n in ftr:
                    ft_[n] = ftr[n].next()
                    k.dma("sp", ft_[n][:], feat[n].rv(["t%d" % st], feat[n].ap.rearrange("(a p) s -> p a s", p=128)[:, :, r0:r0 + 128]))
                return tk_, ft_

            H = Hr.next()
            k.memset("dve", H[:], 0.0)
            Hb = Hbr.next()
            k.memset("pool", Hb[:], 0.0)
            nxt = load_st(0)
            for st in range(NT):
                r0 = st * 128
                tk, ft = nxt
                if st + 1 < NT:
                    nxt = load_st(st + 1)
                for ci, n in enumerate(("V", "P", "Bb", "Kb")):
                    k.copy("pool" if ci % 2 else "act", tkb[n][:], tk[n][:])
                blocks = [("pb", "B", "P", "c_mst"), ("s0", "P", "B", "c_ms"), ("rb", "B", "R", "c_mit"),
                          ("pk", "K", "P", "c_mst"), ("rk", "K", "R", "c_mit")]
                for bi, (nm, ln, rn, mk) in enumerate(blocks):
                    pp = PP[bi % 2]
                    for hp, e, h, hi in HE:
                        kb = 64 * e
                        for c in range(2):
                            ob = 64 * c
                            k.mm(pp[ob:ob + 64, hi * 64:(hi + 1) * 64],
                                 ft[ln][kb:kb + 64, hp, 64 * c:64 * c + 64], ft[rn][kb:kb + 64, hp, 64 * c:64 * c + 64])
                    k.tt("dve", AT[nm][:], v16(pp), bc(cm[mk], 16), ALU.mult)
                for hb in range(2):
                    Sk = AT["s0"][:, hb * 8:(hb + 1) * 8, :]
                    Nk = AT["pb"][:, hb * 8:(hb + 1) * 8, :]
                    Q = Qr.next()
                    k.tt("dve", Q[:], Nk, bc(cm["c_id64"], 8), ALU.add)
                    for lev in range(5):
                        for hh in range(8):
                            for c in range(2):
                                ob = 64 * c
                                k.mm(PP[2][ob:ob + 64, hh * 64:(hh + 1) * 64],
                                     V(Nk.bufs, Nk.ap[ob:ob + 64, hh, :]), V(Sk.bufs, Sk.ap[ob:ob + 64, hh, :]))
                        if lev < 4:
                            for hh in range(8):
                                for c in range(2):
                                    ob = 64 * c
                                    k.mm(PP[2][ob:ob + 64, 512 + hh * 64:512 + (hh + 1) * 64],
                                         V(Sk.bufs, Sk.ap[ob:ob + 64, hh, :]), V(Nk.bufs, Nk.ap[ob:ob + 64, hh, :]))
                        Sn = SN["S"].next()
                        k.copy("act", Sn[:], PP[2].v(PP[2].ap[:, 0:512].rearrange("p (h t) -> p h t", h=8)))
                        if lev < 4:
                            Nn = SN["N"].next()
                            k.copy("dve", Nn[:], PP[2].v(PP[2].ap[:, 512:1024].rearrange("p (h t) -> p h t", h=8)))
                        for hh in range(8):
                            for c in range(2):
                                ob = 64 * c
                                k.mm(PP[3][ob:ob + 64, hh * 64:(hh + 1) * 64], Sn[ob:ob + 64, hh, :], Q[ob:ob + 64, hh, :])
                        if lev < 4:
                            Qn = Qr.next()
                            qdst = Qn[:]
                        else:
                            Qn = None
                            qdst = QT[:, hb * 8:(hb + 1) * 8, :]
                        k.tt("dve", qdst, PP[3].v(PP[3].ap[:, 0:512].rearrange("p (h t) -> p h t", h=8)), Q[:], ALU.add)
                        Sk = Sn[:]
                        if lev < 4:
                            Nk = Nn[:]
                            Q = Qn
                for hp, e, h, hi in HE:
                    for c in range(2):
                        ob = 64 * c
                        k.mm(PP[0][ob:ob + 64, hi * 64:(hi + 1) * 64], AT["pk"][ob:ob + 64, hi, :], tkb["V"][ob:ob + 64, h * 64:(h + 1) * 64])
                k.copy("act", W1[:], v16(PP[0]))
                for hp, e, h, hi in HE:
                    for c in range(2):
                        ob = 64 * c
                        k.mm(PP[1][ob:ob + 64, hi * 64:(hi + 1) * 64], QT[ob:ob + 64, hi, :], W1[ob:ob + 64, hi, :])
                        k.mm(PP[0][ob:ob + 64, hi * 64:(hi + 1) * 64], QT[ob:ob + 64, hi, :], tkb["P"][ob:ob + 64, h * 64:(h + 1) * 64])
                k.copy("act", U0[:], v16(PP[1]))
                k.copy("dve", PH[:], v16(PP[0]))
                k.tt("dve", DG[:], V((cm["c_id64"],), cm["c_id64"].ap.unsqueeze(1).unsqueeze(1).to_broadcast([128, 2, 8, 64])),
                     V((glt,), glt.ap[:, :, 2 * st:2 * st + 2].rearrange("p a c -> p c a").unsqueeze(3).to_broadcast([128, 2, 8, 64])), ALU.mult)
                for hp, e, h, hi in HE:
                    kb = 64 * e
                    hs = slice(h * 64, (h + 1) * 64)
                    for c in range(2):
                        ob = 64 * c
                        sl = c * 512 + hp * 64
                        k.mm(PP[1][kb:kb + 64, sl:sl + 64], PH[ob:ob + 64, hi, :], tkb["Bb"][ob:ob + 64, hs])
                        k.mm(PP[2][kb:kb + 64, sl:sl + 64], tkb["Bb"][ob:ob + 64, hs], U0[ob:ob + 64, hi, :], start=True, stop=False)
                        k.mm(PP[2][kb:kb + 64, sl:sl + 64], tkb["Kb"][ob:ob + 64, hs], tkb["V"][ob:ob + 64, hs], start=False, stop=True)
                        k.mm(PP[3][kb:kb + 64, sl:sl + 64], PH[ob:ob + 64, hi, :], AT["rb"][ob:ob + 64, hi, :])
                        k.mm(PP[0][ob:ob + 64, hi * 64:(hi + 1) * 64], AT["rb"][ob:ob + 64, hi, :], U0[ob:ob + 64, hi, :], start=True, stop=False)
                        k.mm(PP[0][ob:ob + 64, hi * 64:(hi + 1) * 64], AT["rk"][ob:ob + 64, hi, :], tkb["V"][ob:ob + 64, hs], start=False, stop=True)
                k.tt("dve", MT[:], v28(PP[1]), DG[:], ALU.add)
                k.copy("act", NCs[:], v28(PP[2]))
                k.tt("dve", RHT[:], v28(PP[3]), V((ft["R"],), ft["R"].ap.rearrange("p a (c t) -> p c a t", c=2)), ALU.add)
                k.copy("act", Y0[:], v16(PP[0]))
                yt = ysb.next()
                for c in range(2):
                    ob = 64 * c
                    for hp, e, h, hi in HE:
                        kb = 64 * e
                        k.mm(PP[1][ob:ob + 64, hi * 64:(hi + 1) * 64], RHT[kb:kb + 64, c, hp, :], Hb[kb:kb + 64, hp, :])
                    k.tt("dve", yt.v(yt.ap[ob:ob + 64, :].rearrange("p (a e d) -> p e a d", e=2, d=64)),
                         PP[1].v(PP[1].ap[ob:ob + 64, :].rearrange("p (e a d) -> p e a d", e=2, d=64)),
                         Y0.v(Y0.ap[ob:ob + 64].rearrange("p (e a) d -> p e a d", e=2)), ALU.add)
                    for hp, e, h, hi in HE:
                        kb = 64 * e
                        k.mm(PP[2][kb:kb + 64, hp * 64:(hp + 1) * 64], MT[kb:kb + 64, c, hp, :], H[kb:kb + 64, hp, :])
                    Hn = Hr.next()
                    k.tt("dve", Hn[:], PP[2].v(PP[2].ap[:, 0:512].rearrange("p (a t) -> p a t", a=8)), NCs[:, c, :, :], ALU.add)
                    H = Hn
                    Hb = Hbr.next()
                    k.copy("act", Hb[:], Hn[:])
                k.dma("pool", y_d.rv(["t%d" % st], y_d.ap[r0:r0 + 128, :]), yt[:])
            k.barrier()
        k.stack = k.base

    def rw_phase_post(self, j, y_d, raw_g, bv_d, o_d):
        k = self.k
        S, NT = self.S, self.NT
        with ExitStack() as ph:
            k.stack = ph
            prm = {}
            for nm, ext in (("lw", "rw_ln_w"), ("lb", "rw_ln_b")):
                t = k.sb("prm_" + nm, [128, D], F32)
                src = self.ext[ext]
                k.dma("sp", t[:], src.v(src.ap[j:j + 1, :].partition_broadcast(128)))
                prm[nm] = t
            geps = k.sb("geps", [128, 1], F32)
            k.memset("dve", geps[:], GN_EPS)
            yr = Ring(k, "y", 2, [128, D], F32)
            gr = Ring(k, "g", 2, [128, D], BF16)
            br = Ring(k, "bv", 2, [128, D], F32)
            jr = Ring(k, "jk", 1, [128, D], F32)
            sr = Ring(k, "s", 2, [128, 6, 16], F32)
            obr = Ring(k, "ob", 2, [128, D], BF16)
            for st in range(NT):
                r0 = st * 128
                y, g, bv = yr.next(), gr.next(), br.next()
                k.dma("sp", y[:], y_d.rv(["t%d" % st], y_d.ap[r0:r0 + 128, :]))
                k.dma("sp", g[:], raw_g.rv(["t%d" % st], raw_g.ap[r0:r0 + 128, :]))
                k.dma("sp", bv[:], bv_d.rv(["t%d" % st], bv_d.ap[r0:r0 + 128, :]))
                s = sr.next()
                jk = jr.next()
                y3 = y.v(y.ap.rearrange("p (h d) -> p h d", h=16))
                k.reduce(s[:, 0, :], y3, ALU.add)
                k.act(jk[:], y[:], AF.Square)
                k.reduce(s[:, 1, :], jk.v(jk.ap.rearrange("p (h d) -> p h d", h=16)), ALU.add)
                k.ts("dve", s[:, 2, :], s[:, 0, :], 1.0 / 64, ALU.mult)
                k.tt("dve", s[:, 3, :], s[:, 2, :], s[:, 2, :], ALU.mult)
                k.stt(s[:, 4, :], s[:, 1, :], 1.0 / 64, s[:, 3, :], ALU.mult, ALU.subtract)
                k.act(s[:, 5, :], s[:, 4, :], AF.Sqrt, bias=geps[:, 0:1])
                k.recip(s[:, 5, :], s[:, 5, :])
                bcs = lambda i: V(s.v(s.ap).bufs, s.ap[:, i, :].unsqueeze(2).to_broadcast([128, 16, 64]))
                k.tt("dve", y3, y3, bcs(2), ALU.subtract)
                k.tt("dve", y3, y3, bcs(5), ALU.mult)
                k.tt("pool", y[:], y[:], prm["lw"][:], ALU.mult)
                k.tt("pool", y[:], y[:], prm["lb"][:], ALU.add)
                k.tt("dve", y[:], y[:], bv[:], ALU.add)
                o = obr.next()
                k.tt("dve", o[:], y[:], g[:], ALU.mult)
                k.dma("pool", o_d.rv(["g%d" % (st // 4)], o_d.ap[r0:r0 + 128, :]), o[:])
            k.barrier()
        k.stack = k.base

    def build(self):
        k = self.k
        S = self.S
        for li, (kind, j) in enumerate(self.kinds):
            self.cast_weight("gu%d" % li, self.ext["ffn_w_gu"].ap[li], D, 2 * FF)
            self.cast_weight("dn%d" % li, self.ext["ffn_w_down"].ap[li], FF, D)
            if kind == 1:
                self.cast_weight("sbqkv", self.ext["sb_w_qkv"].ap[j], D, 3 * D)
                self.cast_weight("sbout", self.ext["sb_w_out"].ap[j], D, D)
            if kind == 0:
                for s_ in range(3):
                    self.cast_weight("rwrkv%d_%d" % (j, s_), self.ext["rw_w_rkv"].ap[j, s_], D, D)
                self.cast_weight("rwout%d" % j, self.ext["rw_w_out"].ap[j], D, D)
                for nm, r in (("w", 64), ("a", 64), ("g", 160)):
                    self.cast_weight("rw%s1_%d" % (nm, j), self.ext["rw_%s1" % nm].ap[j], D, r)
                    self.cast_weight("rw%s2_%d" % (nm, j), self.ext["rw_%s2" % nm].ap[j], r, D)
                if j >= 1:
                    self.cast_weight("rwv1_%d" % (j - 1), self.ext["rw_v1"].ap[j - 1], D, 32)
                    self.cast_weight("rwv2_%d" % (j - 1), self.ext["rw_v2"].ap[j - 1], 32, D)
            if kind == 2:
                self.cast_weight("dsin", self.ext["ds_w_in"].ap[j], D, 3656)
                self.cast_weight("dsout", self.ext["ds_w_out"].ap[j], D, D)
        xin = self.ext["x"]
        rws = None
        for li, (kind, j) in enumerate(self.kinds):
            if kind == 0:
                if rws is None:
                    rws = {"raw": {n: k.dram("rw_raw_" + n, [S, D], BF16) for n in ("r", "k", "v", "lw", "la", "lg", "lv")},
                           "tok": {n: k.dram("rw_tok_" + n, [S, D], F32) for n in ("V", "P", "Bb", "Kb", "BV")},
                           "feat": {n: k.dram("rw_feat_" + n, [D, S], F32) for n in ("P", "R", "B", "K")},
                           "glt": k.dram("rw_glt", [128, 8, S // 64], F32),
                           "vfirst": k.dram("rw_vfirst", [S, D], BF16),
                           "y": k.dram("rw_y", [S, D], F32),
                           "o": k.dram("rw_o", [S, D], BF16)}
                stop = 9
                self.rw_phase_a1a(li, j, xin, rws["raw"])
                if stop >= 1:
                    self.rw_phase_a1b(li, j, rws["raw"], rws["tok"], rws["feat"], rws["glt"], rws["vfirst"])
                if stop >= 2:
                    self.rw_phase_b(rws["tok"], rws["feat"], rws["glt"], rws["y"])
                if stop >= 3:
                    self.rw_phase_post(j, rws["y"], rws["raw"]["lg"], rws["tok"]["BV"], rws["o"])
                if stop >= 4:
                    self.phase_c(li, rws["o"], "t", "rwout%d" % j, xin)
            if kind == 1:
                qT = k.dram("sb_qT", [D, S], BF16)
                kT = k.dram("sb_kT", [D, S], BF16)
                vP = k.dram("sb_vP", [8, 128, self.NT, 128], BF16)
                oT = k.dram("sb_oT", [D, S], BF16)
                self.sb_phase_a(li, j, xin, qT, kT, vP)
                self.sb_phase_b(qT, kT, vP, oT)
                self.phase_c(li, oT, "f", "sbout", xin)
            if kind == 2:
                NT = self.NT
                qT = k.dram("ds_qT", [D, S], BF16)
                kT = k.dram("ds_kT", [D, S], BF16)
                vP = k.dram("ds_vP", [8, 128, NT, 128], BF16)
                qiT = k.dram("ds_qiT", [512, S], BF16)
                kiT = k.dram("ds_kiT", [64, S], BF16)
                wiS = k.dram("ds_wi", [S, 8], F32)
                maskT = k.dram("ds_maskT", [128, NT * (NT + 1) // 2, 128], BF16)
                o_d = k.dram("ds_o", [S, D], BF16)
                self.ds_phase_a(li, j, xin, qT, kT, vP, qiT, kiT, wiS)
                self.ds_phase_b1(qiT, kiT, wiS, maskT)
                self.ds_phase_b2(qT, kT, vP, maskT, o_d)
                self.phase_c(li, o_d, "t", "dsout", xin)
            xin = self.y
        k.barrier()
        print("program: ninst=%d nwait=%d" % (k.ninst, k.nwait))
        return self.nc


W_NAMES = ["norm_mix", "norm_ffn", "ffn_w_gu", "ffn_w_down", "rw_mix", "rw_w_rkv", "rw_w0", "rw_w1", "rw_w2",
           "rw_a0", "rw_a1", "rw_a2", "rw_g1", "rw_g2", "rw_v0", "rw_v1", "rw_v2", "rw_k_k", "rw_k_a", "rw_r_k",
           "rw_ln_w", "rw_ln_b", "rw_w_out", "sb_w_qkv", "sb_w_out", "ds_w_in", "ds_q_norm", "ds_k_norm", "ds_w_out"]


def run_model(inputs, kinds):
    x = np.ascontiguousarray(inputs["x"], dtype=np.float32)
    B, S, _ = x.shape
    wspecs = {"x": (S, D)}
    for n in W_NAMES:
        wspecs[n] = tuple(inputs[n].shape)
    prog = Prog(S, kinds, wspecs)
    nc = prog.build()
    consts = Prog.const_values(S)
    shared = {n: np.ascontiguousarray(inputs[n], dtype=np.float32) for n in W_NAMES}
    shared.update(consts)
    in_maps = []
    for c in range(8):
        m = dict(shared)
        m["x"] = x[(c // 2) % B]
        in_maps.append(m)
    res = run_bass_kernel_spmd(nc, in_maps, core_ids=list(range(8)))
    out = np.stack([np.asarray(res.results[2 * b]["y"], dtype=np.float32) for b in range(B)] if B == 4 else
                   [np.asarray(res.results[(2 * b) % 8]["y"], dtype=np.float32) for b in range(B)])
    return out


def kernel(**inputs):
    kinds = [(i % 3, i // 3) for i in range(4)]
    return run_model(inputs, kinds)
```

```python
from contextlib import ExitStack
import numpy as np
import ml_dtypes
import concourse.bass as bass
import concourse.mybir as mybir
from concourse.bass_utils import run_bass_kernel_spmd

F32 = mybir.dt.float32
BF16 = mybir.dt.bfloat16
AF = mybir.ActivationFunctionType
ALU = mybir.AluOpType
AX = mybir.AxisListType

D = 1024
NH = 16
DH = 64
FF = 2816
EPS = 1e-6
GN_EPS = 64 * 1e-5
TG = 512
NEG = -1.0e30


class V:
    __slots__ = ("bufs", "ap")

    def __init__(self, bufs, ap):
        self.bufs = bufs
        self.ap = ap


class Buf:
    __slots__ = ("name", "w", "r", "ap", "regs")

    def __init__(self, name, ap=None):
        self.name = name
        self.w = None
        self.r = {}
        self.ap = ap
        self.regs = {}

    def __getitem__(self, idx):
        return V((self,), self.ap[idx])

    def v(self, ap):
        return V((self,), ap)

    def reg(self, key):
        b = self.regs.get(key)
        if b is None:
            b = Buf("%s/%s" % (self.name, key), self.ap)
            self.regs[key] = b
        return b

    def rv(self, keys, ap):
        return V(tuple(self.reg(k) for k in keys), ap)


def _bufs(vs):
    out = []
    for v in vs:
        if v is None or isinstance(v, (int, float)):
            continue
        out.extend(v.bufs)
    return out


class K:
    def __init__(self, nc, n_dma_sems=8):
        self.nc = nc
        self.base = ExitStack()
        self.stack = self.base
        self.eng = {"pe": nc.tensor, "act": nc.scalar, "dve": nc.vector,
                    "pool": nc.gpsimd, "sp": nc.sync}
        self.sems = {}
        self.cnt = {}
        self.seen = {e: {} for e in self.eng}
        for e in self.eng:
            self.sems[e] = self.base.enter_context(nc.semaphore("c_" + e))
            self.cnt[e] = 0
        self.dsem = {}
        self.dcnt = {}
        self.dptr = {}
        for q in ("sp", "pool"):
            self.dsem[q] = []
            for i in range(n_dma_sems):
                key = "d_%s%d" % (q, i)
                self.sems[key] = self.base.enter_context(nc.semaphore(key))
                self.dsem[q].append(key)
                self.dcnt[key] = 0
            self.dptr[q] = 0
        self.ninst = 0
        self.nwait = 0
        self.uid = 0

    def sb(self, name, shape, dt=F32):
        self.uid += 1
        nm = "%s_%d" % (name, self.uid)
        t = self.stack.enter_context(self.nc.sbuf_tensor(nm, list(shape), dt))
        ap = t[:]
        assert tuple(ap.shape) == tuple(shape), (ap.shape, shape)
        return Buf(nm, ap)

    def ps(self, name, shape, dt=F32):
        self.uid += 1
        nm = "%s_%d" % (name, self.uid)
        t = self.stack.enter_context(self.nc.psum_tensor(nm, list(shape), dt))
        ap = t[:]
        assert tuple(ap.shape) == tuple(shape), (ap.shape, shape)
        return Buf(nm, ap)

    def dram(self, name, shape, dt=F32, kind="Internal"):
        t = self.nc.dram_tensor(name, list(shape), dt, kind=kind)
        return Buf(name, t.ap())

    def _wait(self, e, ev):
        key, val = ev
        if self.seen[e].get(key, 0) >= val:
            return
        self.eng[e].wait_ge(self.sems[key], val)
        self.seen[e][key] = val
        self.nwait += 1

    def _deps(self, e, reads, writes, pe_accum=False):
        for b in reads:
            if b.w is not None:
                self._wait(e, b.w)
        for b in writes:
            if b.w is not None and not (pe_accum and b.w[0] == "pe"):
                self._wait(e, b.w)
            for key, val in b.r.items():
                self._wait(e, (key, val))

    def _mark(self, ev, reads, writes):
        for b in reads:
            if b.r.get(ev[0], 0) < ev[1]:
                b.r[ev[0]] = ev[1]
        for b in writes:
            b.w = ev
            b.r = {}

    def op(self, e, issue, reads=(), writes=(), pe_accum=False):
        reads = _bufs(reads)
        writes = _bufs(writes)
        self._deps(e, reads, writes, pe_accum)
        inst = issue(self.eng[e])
        self.cnt[e] += 1
        inst.then_inc(self.sems[e], 1)
        ev = (e, self.cnt[e])
        self._mark(ev, reads, writes)
        self.ninst += 1
        return ev

    def dma(self, q, out, in_, **kw):
        reads = _bufs([in_])
        writes = _bufs([out])
        self._deps(q, reads, writes)
        key = self.dsem[q][self.dptr[q]]
        self.dptr[q] = (self.dptr[q] + 1) % len(self.dsem[q])
        if self.dcnt[key] > 0:
            self._wait(q, (key, self.dcnt[key]))
        inst = self.eng[q].dma_start(out=out.ap, in_=in_.ap, **kw)
        self.dcnt[key] += 16
        inst.then_inc(self.sems[key], 16)
        ev = (key, self.dcnt[key])
        self._mark(ev, reads, writes)
        self.ninst += 1
        return ev

    def barrier(self):
        evs = [(e, self.cnt[e]) for e in self.eng if self.cnt[e] > 0]
        evs += [(key, c) for key, c in self.dcnt.items() if c > 0]
        for e in self.eng:
            for ev in evs:
                if ev[0] != e:
                    self._wait(e, ev)

    def mm(self, out, lhsT, rhs, start=True, stop=True):
        return self.op("pe", lambda e: e.matmul(out.ap, lhsT.ap, rhs.ap, start=start, stop=stop),
                       reads=[lhsT, rhs], writes=[out], pe_accum=True)

    def tr(self, out, in_, ident):
        return self.op("pe", lambda e: e.transpose(out.ap, in_.ap, ident.ap),
                       reads=[in_, ident], writes=[out], pe_accum=True)

    def act(self, out, in_, func, scale=None, bias=None, accum=None):
        kw = {}
        rd = [in_]
        if scale is not None:
            kw["scale"] = scale.ap if isinstance(scale, V) else scale
            rd.append(scale)
        if bias is not None:
            kw["bias"] = bias.ap if isinstance(bias, V) else bias
            rd.append(bias)
        wr = [out]
        if accum is not None:
            kw["accum_out"] = accum.ap
            wr.append(accum)
        return self.op("act", lambda e: e.activation(out=out.ap, in_=in_.ap, func=func, **kw),
                       reads=rd, writes=wr)

    def tt(self, e, out, a, b, op):
        return self.op(e, lambda g: g.tensor_tensor(out=out.ap, in0=a.ap, in1=b.ap, op=op),
                       reads=[a, b], writes=[out])

    def ts(self, e, out, a, s1, op0, s2=None, op1=None, accum=None):
        kw = {}
        wr = [out]
        if op1 is not None:
            kw["op1"] = op1
        if accum is not None:
            kw["accum_out"] = accum.ap
            wr.append(accum)
        sa1 = s1.ap if isinstance(s1, V) else s1
        sa2 = s2.ap if isinstance(s2, V) else s2
        return self.op(e, lambda g: g.tensor_scalar(out=out.ap, in0=a.ap, scalar1=sa1, scalar2=sa2,
                                                    op0=op0, **kw),
                       reads=[a, s1, s2], writes=wr)

    def stt(self, out, in0, scalar, in1, op0, op1):
        sa = scalar.ap if isinstance(scalar, V) else scalar
        return self.op("dve", lambda g: g.scalar_tensor_tensor(out=out.ap, in0=in0.ap, scalar=sa,
                                                               in1=in1.ap, op0=op0, op1=op1),
                       reads=[in0, scalar, in1], writes=[out])

    def copy(self, e, out, in_):
        if e == "act":
            return self.op("act", lambda g: g.copy(out=out.ap, in_=in_.ap), reads=[in_], writes=[out])
        return self.op(e, lambda g: g.tensor_copy(out=out.ap, in_=in_.ap), reads=[in_], writes=[out])

    def memset(self, e, out, val):
        return self.op(e, lambda g: g.memset(out.ap, val), reads=[], writes=[out])

    def reduce(self, out, in_, op, axis=AX.X):
        return self.op("dve", lambda g: g.tensor_reduce(out=out.ap, in_=in_.ap, axis=axis, op=op),
                       reads=[in_], writes=[out])

    def recip(self, out, in_):
        return self.op("dve", lambda g: g.reciprocal(out=out.ap, in_=in_.ap), reads=[in_], writes=[out])


class Ring:
    def __init__(self, k, name, n, shape, dt, psum=False):
        self.bufs = [(k.ps if psum else k.sb)("%s%d" % (name, i), shape, dt) for i in range(n)]
        self.i = 0

    def next(self):
        b = self.bufs[self.i]
        self.i = (self.i + 1) % len(self.bufs)
        return b


class Prog:
    def __init__(self, S, kinds, wspecs):
        self.S = S
        self.kinds = kinds
        self.NT = S // 128
        self.NG = S // TG
        nc = bass.Bass("TRN2", target_bir_lowering=False)
        self.nc = nc
        self.k = K(nc)
        k = self.k
        self.ext = {}
        for name, shp in wspecs.items():
            self.ext[name] = Buf(name, nc.dram_tensor(name, list(shp), F32, kind="ExternalInput").ap())
        self.cext = {}
        for name, (shp, dt) in self.const_specs().items():
            self.cext[name] = Buf(name, nc.dram_tensor(name, list(shp), dt, kind="ExternalInput").ap())
        self.y = Buf("y", nc.dram_tensor("y", [S, D], F32, kind="ExternalOutput").ap())
        self.wb = {}

    def const_specs(self):
        S = self.S
        return {
            "c_identb": ([128, 128], BF16),
            "c_identf": ([128, 128], F32),
            "c_uincl": ([128, 128], BF16),
            "c_ones": ([128, 128], BF16),
            "c_sbmask": ([4, 128, 512], F32),
            "c_tribd": ([128, 128], F32),
            "c_onesbd": ([128, 128], F32),
            "c_id64": ([128, 64], F32),
            "c_mst": ([128, 64], F32),
            "c_mit": ([128, 64], F32),
            "c_ms": ([128, 64], F32),
            "c_cos": ([S, 32], F32),
            "c_sin": ([S, 32], F32),
        }

    @staticmethod
    def const_values(S):
        bf = ml_dtypes.bfloat16
        idx = np.arange(128)
        c = {}
        c["c_identb"] = np.eye(128, dtype=np.float32).astype(bf)
        c["c_identf"] = np.eye(128, dtype=np.float32)
        c["c_uincl"] = (idx[:, None] >= idx[None, :]).astype(np.float32).astype(bf)
        c["c_ones"] = np.ones((128, 128), np.float32).astype(bf)
        tq = np.arange(512)
        c["c_sbmask"] = np.stack([(tq[None, :] > (idx[:, None] + 128 * i)).astype(np.float32)
                                  for i in range(4)])
        same = (idx[:, None] // 64) == (idx[None, :] // 64)
        c["c_tribd"] = (same & (idx[:, None] <= idx[None, :])).astype(np.float32)
        c["c_onesbd"] = same.astype(np.float32)
        p64 = (idx % 64)[:, None]
        t64 = np.arange(64)[None, :]
        c["c_id64"] = (p64 == t64).astype(np.float32)
        c["c_mst"] = (t64 > p64).astype(np.float32)
        c["c_mit"] = (t64 >= p64).astype(np.float32)
        c["c_ms"] = (t64 < p64).astype(np.float32)
        half = 32
        inv = 1.0 / (10000.0 ** (np.arange(half, dtype=np.float32) / half))
        ang = np.arange(S, dtype=np.float32)[:, None] * inv[None, :]
        c["c_cos"] = np.cos(ang).astype(np.float32)
        c["c_sin"] = np.sin(ang).astype(np.float32)
        return c

    def cast_weight(self, key, src_ap, Kd, N):
        k = self.k
        dst = k.dram("wb_" + key, [Kd, N], BF16)
        src = Buf("src_" + key, src_ap)
        rows = 256
        for r0 in range(0, Kd, rows):
            r1 = min(Kd, r0 + rows)
            k.dma("pool", dst[r0:r1, :], src[r0:r1, :], max_dma_last_dim=4096)
        self.wb[key] = (dst, Kd, N)

    def wsrc(self, key, c0, cw):
        dst, Kd, N = self.wb[key]
        if Kd % 128 == 0:
            return dst.v(dst.ap.rearrange("(kc p) n -> p kc n", p=128)[:, :, c0:c0 + cw])
        return dst[:, c0:c0 + cw]

    def norm_transpose(self, xt, gam, hT, sub, pools, col0=0):
        k = self.k
        junk, ssb, hb, pT = pools["junk"].next(), pools["ss"].next(), pools["hb"].next(), pools["pT"].next()
        k.act(junk[:], xt, AF.Square, accum=ssb[:, 0:1])
        k.act(ssb[:, 1:2], ssb[:, 0:1], AF.Sqrt, scale=1.0 / D, bias=pools["eps"][:, 0:1])
        k.recip(ssb[:, 2:3], ssb[:, 1:2])
        k.stt(hb[:], xt, ssb[:, 2:3], gam, ALU.mult, ALU.mult)
        for kc in range(8):
            k.tr(pT[:, kc * 128:(kc + 1) * 128], hb[:, kc * 128:(kc + 1) * 128], pools["identb"][:])
        c = col0 + sub * 128
        k.copy("dve", hT[:, :, c:c + 128], pT.v(pT.ap.rearrange("p (k t) -> p k t", k=8)))

    def phase_c(self, li, o_buf, o_major, wout_key, xin):
        k = self.k
        S, NG = self.S, self.NG
        with ExitStack() as ph:
            k.stack = ph
            identb = k.sb("identb", [128, 128], BF16)
            k.dma("sp", identb[:], self.cext["c_identb"][:])
            eps = k.sb("eps", [128, 1], F32)
            k.memset("dve", eps[:], EPS)
            gam = k.sb("gam", [128, D], F32)
            k.dma("sp", gam[:], self.ext["norm_ffn"].v(self.ext["norm_ffn"].ap[li:li + 1, :].partition_broadcast(128)))
            pools = {"junk": Ring(k, "junk", 1, [128, D], F32), "ss": Ring(k, "ss", 2, [128, 4], F32),
                     "hb": Ring(k, "hb", 2, [128, D], BF16), "pT": Ring(k, "pT", 1, [128, D], BF16, psum=True),
                     "identb": identb, "eps": eps}
            oT = Ring(k, "oT", 1, [128, 8, TG], BF16)
            ot = Ring(k, "ot", 2, [128, D], BF16)
            wout = Ring(k, "wout", 2, [128, 8, 512], BF16)
            x1 = [k.sb("x1_%d" % i, [128, D], F32) for i in range(4)]
            xi = Ring(k, "xi", 2, [128, D], F32)
            h2T = Ring(k, "h2T", 1, [128, 8, TG], BF16)
            wgu = Ring(k, "wgu", 3, [128, 8, 512], BF16)
            aT = Ring(k, "aT", 1, [128, 22, TG], BF16)
            wd = Ring(k, "wd", 2, [128, 22, 512], BF16)
            sg = Ring(k, "sg", 2, [128, TG], F32)
            xo = Ring(k, "xo", 2, [128, D], F32)
            pacc = Ring(k, "pacc", 4, [128, 512], F32, psum=True)
            for g in range(NG):
                t0 = g * TG
                oTt = oT.next()
                if o_major == "f":
                    src = o_buf.rv(["g%d" % g], o_buf.ap.rearrange("(kc p) s -> p kc s", p=128)[:, :, t0:t0 + TG])
                    k.dma("sp", oTt[:], src)
                else:
                    for sub in range(4):
                        ott = ot.next()
                        r0 = t0 + sub * 128
                        k.dma("sp", ott[:], o_buf.rv(["g%d" % g], o_buf.ap[r0:r0 + 128, :]))
                        pT = pools["pT"].next()
                        for kc in range(8):
                            k.tr(pT[:, kc * 128:(kc + 1) * 128], ott[:, kc * 128:(kc + 1) * 128], identb[:])
                        k.copy("dve", oTt[:, :, sub * 128:(sub + 1) * 128],
                               pT.v(pT.ap.rearrange("p (k t) -> p k t", k=8)))
                wo = [wout.next(), wout.next()]
                for nb in range(2):
                    k.dma("sp", wo[nb][:], self.wsrc(wout_key, nb * 512, 512))
                h2 = h2T.next()
                for sub in range(4):
                    r0 = t0 + sub * 128
                    xt = xi.next()
                    k.dma("sp", xt[:], xin.rv(["t%d" % (r0 // 128)], xin.ap[r0:r0 + 128, :]))
                    for nb in range(2):
                        pa = pacc.next()
                        for kc in range(8):
                            k.mm(pa[:], oTt[:, kc, sub * 128:(sub + 1) * 128], wo[nb][:, kc, :],
                                 start=(kc == 0), stop=(kc == 7))
                        k.tt("dve", x1[sub][:, nb * 512:(nb + 1) * 512], pa[:], xt[:, nb * 512:(nb + 1) * 512], ALU.add)
                    self.norm_transpose(x1[sub][:], gam[:], h2, sub, pools)
                a = aT.next()
                for fb in range(11):
                    w = wgu.next()
                    k.dma("sp", w[:, :, 0:256], self.wsrc("gu%d" % li, fb * 256, 256))
                    k.dma("sp", w[:, :, 256:512], self.wsrc("gu%d" % li, FF + fb * 256, 256))
                    for jj in range(2):
                        j = 2 * fb + jj
                        pg = pacc.next()
                        pu = pacc.next()
                        for kc in range(8):
                            k.mm(pg[:], w[:, kc, jj * 128:(jj + 1) * 128], h2[:, kc, :], start=(kc == 0), stop=(kc == 7))
                        for kc in range(8):
                            k.mm(pu[:], w[:, kc, 256 + jj * 128:256 + (jj + 1) * 128], h2[:, kc, :],
                                 start=(kc == 0), stop=(kc == 7))
                        s = sg.next()
                        k.act(s[:], pg[:], AF.Silu)
                        k.tt("dve", a[:, j, :], s[:], pu[:], ALU.mult)
                wds = [wd.next(), wd.next()]
                for nb in range(2):
                    k.dma("sp", wds[nb][:], self.wsrc("dn%d" % li, nb * 512, 512))
                for sub in range(4):
                    r0 = t0 + sub * 128
                    xot = xo.next()
                    for nb in range(2):
                        pa = pacc.next()
                        for j in range(22):
                            k.mm(pa[:], a[:, j, sub * 128:(sub + 1) * 128], wds[nb][:, j, :],
                                 start=(j == 0), stop=(j == 21))
                        k.tt("dve", xot[:, nb * 512:(nb + 1) * 512], pa[:], x1[sub][:, nb * 512:(nb + 1) * 512], ALU.add)
                    k.dma("pool", self.y.rv(["t%d" % (r0 // 128)], self.y.ap[r0:r0 + 128, :]), xot[:])
            k.barrier()
        k.stack = k.base

    def front_pools(self, li, gam_name):
        k = self.k
        identb = k.sb("identb", [128, 128], BF16)
        k.dma("sp", identb[:], self.cext["c_identb"][:])
        eps = k.sb("eps", [128, 1], F32)
        k.memset("dve", eps[:], EPS)
        gam = k.sb("gam", [128, D], F32)
        src = self.ext[gam_name]
        k.dma("sp", gam[:], src.v(src.ap[li:li + 1, :].partition_broadcast(128)))
        pools = {"junk": Ring(k, "junk", 1, [128, D], F32), "ss": Ring(k, "ss", 2, [128, 4], F32),
                 "hb": Ring(k, "hb", 2, [128, D], BF16), "pT": Ring(k, "pT", 1, [128, D], BF16, psum=True),
                 "identb": identb, "eps": eps, "gam": gam}
        return pools

    def sb_phase_a(self, li, j, xin, qT, kT, vP):
        k = self.k
        S, NG = self.S, self.NG
        with ExitStack() as ph:
            k.stack = ph
            pools = self.front_pools(li, "norm_mix")
            xi = Ring(k, "xi", 2, [128, D], F32)
            hTr = Ring(k, "hT", 2, [128, 8, TG], BF16)
            wr = Ring(k, "w", 3, [128, 8, 512], BF16)
            ob = Ring(k, "ob", 3, [128, 512], BF16)
            pacc = Ring(k, "pacc", 4, [128, 512], F32, psum=True)
            cp = 0
            for g in range(NG):
                t0 = g * TG
                hT = hTr.next()
                for sub in range(4):
                    r0 = t0 + sub * 128
                    xt = xi.next()
                    k.dma("sp", xt[:], xin.rv(["t%d" % (r0 // 128)], xin.ap[r0:r0 + 128, :]))
                    self.norm_transpose(xt[:], pools["gam"][:], hT, sub, pools)
                for blk in range(6):
                    w = wr.next()
                    k.dma("sp", w[:], self.wsrc("sbqkv", blk * 512, 512))
                    if blk < 4:
                        dst = qT if blk < 2 else kT
                        for cc in range(4):
                            pa = pacc.next()
                            for kc in range(8):
                                k.mm(pa[:], w[:, kc, cc * 128:(cc + 1) * 128], hT[:, kc, :], start=(kc == 0), stop=(kc == 7))
                            o = ob.next()
                            k.copy("act" if cp % 2 else "dve", o[:], pa[:])
                            cp += 1
                            row = ((blk % 2) * 4 + cc) * 128
                            k.dma("pool", dst.rv(["g%d" % g], dst.ap[row:row + 128, t0:t0 + TG]), o[:])
                    else:
                        nb = blk - 4
                        for sub in range(4):
                            pa = pacc.next()
                            for kc in range(8):
                                k.mm(pa[:], hT[:, kc, sub * 128:(sub + 1) * 128], w[:, kc, :], start=(kc == 0), stop=(kc == 7))
                            o = ob.next()
                            k.copy("act" if cp % 2 else "dve", o[:], pa[:])
                            cp += 1
                            tix = g * 4 + sub
                            k.dma("pool", vP.rv(["g%d" % g], vP.ap[nb * 4:(nb + 1) * 4, :, tix, :].rearrange("h p c -> p h c")),
                                  o.v(o.ap.rearrange("p (h c) -> p h c", h=4)))
            k.barrier()
        k.stack = k.base

    def sb_phase_b(self, qT, kT, vP, oT, NHP=8):
        k = self.k
        S, NG, NT = self.S, self.NG, self.NT
        allg = ["g%d" % g for g in range(NG)]
        with ExitStack() as ph:
            k.stack = ph
            uincl = k.sb("uincl", [128, 128], BF16)
            ones = k.sb("ones", [128, 128], BF16)
            k.dma("sp", uincl[:], self.cext["c_uincl"][:])
            k.dma("sp", ones[:], self.cext["c_ones"][:])
            msk = k.sb("msk", [128, 4, 512], F32)
            k.dma("sp", msk[:], self.cext["c_sbmask"].v(self.cext["c_sbmask"].ap.rearrange("i s t -> s i t")))
            kTp = k.sb("kTp", [128, S], BF16)
            qTp = k.sb("qTp", [128, S], BF16)
            vp = k.sb("vp", [128, NT, 128], BF16)
            zr = Ring(k, "z", 3, [128, 512], F32, psum=True)
            cr = Ring(k, "c", 3, [128, 512], F32, psum=True)
            orr = Ring(k, "o", 2, [128, 512], F32, psum=True)
            er = Ring(k, "e", 5, [128, 512], F32)
            spr = Ring(k, "sp", 4, [128, 512], BF16)
            sumr = Ring(k, "sum", 4, [128, 512], BF16)
            ecr = Ring(k, "ec", 3, [128, 512], F32)
            atr = Ring(k, "att", 4, [128, 512], BF16)
            osb = Ring(k, "osb", 2, [64, 512], BF16)
            def s0(job):
                hp, e, qb, kt, first, last = job["key"]
                pb = 64 * e
                z = zr.next()
                k.mm(z[:], kTp[pb:pb + 64, kt * 128:(kt + 1) * 128], qTp[pb:pb + 64, qb * 512:(qb + 1) * 512])
                job["z"] = z

            def s1(job):
                hp, e, qb, kt, first, last = job["key"]
                et = er.next()
                k.act(et[:], job["z"][:], AF.Exp, scale=0.125)
                i = kt - 4 * qb
                if i >= 0:
                    k.tt("dve", et[:], et[:], msk[:, i, :], ALU.mult)
                spt = spr.next()
                k.act(spt[:], et[:], AF.Ln, bias=1.0)
                job["et"] = et
                job["spt"] = spt

            def s2(job):
                hp, e, qb, kt, first, last = job["key"]
                c = cr.next()
                spt = job["spt"]
                k.mm(c[:], uincl[:], spt[:], start=True, stop=first)
                if not first:
                    k.mm(c[:], ones[:], job["ssum_in"][:], start=False, stop=True)
                if first:
                    k.copy("pool", job["ssum_out"][:], spt[:])
                elif not last:
                    k.tt("pool", job["ssum_out"][:], job["ssum_in"][:], spt[:], ALU.add)
                job["c"] = c

            def s3(job):
                ec = ecr.next()
                k.act(ec[:], job["c"][:], AF.Exp, scale=-1.0)
                at = atr.next()
                k.tt("dve", at[:], job["et"][:], ec[:], ALU.mult)
                job["at"] = at

            def s4(job):
                hp, e, qb, kt, first, last = job["key"]
                pb = 64 * e
                oacc = job["oacc"]
                k.mm(oacc[0:64, :], vp[:, kt, pb:pb + 64], job["at"][:], start=first, stop=last)
                if last:
                    o = osb.next()
                    k.copy("dve", o[:], oacc[0:64, :])
                    row = hp * 128 + pb
                    k.dma("pool", oT.rv(["g%d" % qb], oT.ap[row:row + 64, qb * 512:(qb + 1) * 512]), o[:])

            stages = [s0, s1, s2, s3, s4]
            for hp in range(NHP):
                k.dma("sp", kTp[:], kT.rv(allg, kT.ap[hp * 128:(hp + 1) * 128, :]))
                k.dma("sp", qTp[:], qT.rv(allg, qT.ap[hp * 128:(hp + 1) * 128, :]))
                k.dma("sp", vp[:], vP.rv(allg, vP.ap[hp]))
                jobs = []
                for e in range(2):
                    for qb in range(NG):
                        oacc = orr.next()
                        kts = list(range(4 * qb + 3, -1, -1))
                        prev = None
                        for n, kt in enumerate(kts):
                            cur = sumr.next()
                            jobs.append({"key": (hp, e, qb, kt, n == 0, n == len(kts) - 1), "oacc": oacc,
                                         "ssum_in": prev, "ssum_out": cur})
                            prev = cur
                for n in range(len(jobs) + len(stages) - 1):
                    for si, fn in enumerate(stages):
                        if 0 <= n - si < len(jobs):
                            fn(jobs[n - si])
            k.barrier()
        k.stack = k.base

    def rope(self, src3, tabs, out3, nh, tmp):
        k = self.k
        A, Bm, Cc, Dd = [V(t.bufs, t.ap.unsqueeze(1).to_broadcast([128, nh, 32])) for t in tabs]
        x1 = V(src3.bufs, src3.ap[:, :, 0:32])
        x2 = V(src3.bufs, src3.ap[:, :, 32:64])
        t1, t2, t3, t4 = [t[:, 0:nh, :] for t in tmp]
        k.tt("dve", t1, x1, A, ALU.mult)
        k.tt("dve", t2, x2, Bm, ALU.mult)
        k.tt("dve", V(out3.bufs, out3.ap[:, :, 0:32]), t1, t2, ALU.subtract)
        k.tt("pool", t3, x1, Cc, ALU.mult)
        k.tt("pool", t4, x2, Dd, ALU.mult)
        k.tt("pool", V(out3.bufs, out3.ap[:, :, 32:64]), t3, t4, ALU.add)

    def ds_phase_a(self, li, j, xin, qT, kT, vP, qiT, kiT, wiS):
        k = self.k
        S, NG = self.S, self.NG
        with ExitStack() as ph:
            k.stack = ph
            pools = self.front_pools(li, "norm_mix")
            identb = pools["identb"]
            xi = Ring(k, "xi", 2, [128, D], F32)
            hTr = Ring(k, "hT", 1, [128, 8, TG], BF16)
            wr = Ring(k, "w", 3, [128, 8, 512], BF16)
            ob = Ring(k, "ob", 3, [128, 512], BF16)
            pacc = Ring(k, "pacc", 4, [128, 512], F32, psum=True)
            qk = [[k.sb("qk%d_%d" % (w_, s_), [128, D], F32) for s_ in range(4)] for w_ in range(2)]
            qis = [k.sb("qis%d" % s_, [128, 512], F32) for s_ in range(4)]
            kws = [k.sb("kws%d" % s_, [128, 72], F32) for s_ in range(4)]
            gqk = k.sb("gqk", [128, 2, 64], F32)
            for w_, nm in enumerate(("ds_q_norm", "ds_k_norm")):
                src = self.ext[nm]
                k.dma("sp", gqk[:, w_, :], src.v(src.ap[j:j + 1, :].partition_broadcast(128)))
            csr = Ring(k, "cs", 2, [128, 2, 32], F32)
            tabr = Ring(k, "tab", 2, [128, 2, 4, 32], F32)
            tmp = [k.sb("rtmp%d" % i, [128, 16, 32], F32) for i in range(4)]
            ssr = Ring(k, "hss", 2, [128, 3, 16], F32)
            rb = Ring(k, "rb", 2, [128, D], BF16)
            rbi = Ring(k, "rbi", 2, [128, 512 + 64], BF16)
            wir = Ring(k, "wir", 2, [128, 8], F32)
            qTg = [k.sb("qTg%d" % w_, [128, 8, TG], BF16) for w_ in range(2)]
            qiTg = k.sb("qiTg", [128, 4, TG], BF16)
            kiTg = k.sb("kiTg", [64, TG], BF16)
            junk = pools["junk"]
            cp = 0
            for g in range(NG):
                t0 = g * TG
                hT = hTr.next()
                for sub in range(4):
                    r0 = t0 + sub * 128
                    xt = xi.next()
                    k.dma("sp", xt[:], xin.rv(["t%d" % (r0 // 128)], xin.ap[r0:r0 + 128, :]))
                    self.norm_transpose(xt[:], pools["gam"][:], hT, sub, pools)
                for blk in range(8):
                    cw = 512 if blk < 7 else 72
                    w = wr.next()
                    k.dma("sp", w[:, :, 0:cw], self.wsrc("dsin", blk * 512, cw))
                    for sub in range(4):
                        pa = pacc.next()
                        for kc in range(8):
                            k.mm(pa[:, 0:cw], hT[:, kc, sub * 128:(sub + 1) * 128], w[:, kc, 0:cw], start=(kc == 0), stop=(kc == 7))
                        eng = "act" if cp % 2 else "dve"
                        cp += 1
                        if blk < 4:
                            k.copy(eng, qk[blk // 2][sub][:, (blk % 2) * 512:(blk % 2 + 1) * 512], pa[:])
                        elif blk < 6:
                            nb = blk - 4
                            o = ob.next()
                            k.copy(eng, o[:], pa[:])
                            tix = g * 4 + sub
                            k.dma("pool", vP.rv(["g%d" % g], vP.ap[nb * 4:(nb + 1) * 4, :, tix, :].rearrange("h p c -> p h c")),
                                  o.v(o.ap.rearrange("p (h c) -> p h c", h=4)))
                        elif blk == 6:
                            k.copy(eng, qis[sub][:], pa[:])
                        else:
                            k.copy(eng, kws[sub][:], pa[:, 0:72])
                for sub in range(4):
                    r0 = t0 + sub * 128
                    cs = csr.next()
                    k.dma("sp", cs[:, 0, :], self.cext["c_cos"][r0:r0 + 128, :])
                    k.dma("sp", cs[:, 1, :], self.cext["c_sin"][r0:r0 + 128, :])
                    tab = tabr.next()
                    for w_ in range(2):
                        k.tt("dve", tab[:, w_, 0, :], cs[:, 0, :], gqk[:, w_, 0:32], ALU.mult)
                        k.tt("dve", tab[:, w_, 1, :], cs[:, 1, :], gqk[:, w_, 32:64], ALU.mult)
                        k.tt("dve", tab[:, w_, 2, :], cs[:, 1, :], gqk[:, w_, 0:32], ALU.mult)
                        k.tt("dve", tab[:, w_, 3, :], cs[:, 0, :], gqk[:, w_, 32:64], ALU.mult)
                    for w_ in range(2):
                        src = qk[w_][sub]
                        jk = junk.next()
                        ss = ssr.next()
                        k.act(jk[:], src[:], AF.Square)
                        k.reduce(ss[:, 0, :], jk.v(jk.ap.rearrange("p (h d) -> p h d", h=16)), ALU.add)
                        k.act(ss[:, 1, :], ss[:, 0, :], AF.Sqrt, scale=1.0 / 64, bias=pools["eps"][:, 0:1])
                        k.recip(ss[:, 2, :], ss[:, 1, :])
                        s3 = src.v(src.ap.rearrange("p (h d) -> p h d", h=16))
                        k.tt("dve", s3, s3, V(ss.v(ss.ap).bufs, ss.ap[:, 2, :].unsqueeze(2).to_broadcast([128, 16, 64])), ALU.mult)
                        r = rb.next()
                        self.rope(s3, [tab[:, w_, i, :] for i in range(4)], r.v(r.ap.rearrange("p (h d) -> p h d", h=16)), 16, tmp)
                        pT = pools["pT"].next()
                        for kc in range(8):
                            k.tr(pT[:, kc * 128:(kc + 1) * 128], r[:, kc * 128:(kc + 1) * 128], identb[:])
                        k.copy("act", qTg[w_][:, :, sub * 128:(sub + 1) * 128], pT.v(pT.ap.rearrange("p (k t) -> p k t", k=8)))
                    ri = rbi.next()
                    plain = [cs[:, 0, :], cs[:, 1, :], cs[:, 1, :], cs[:, 0, :]]
                    q3 = qis[sub].v(qis[sub].ap.rearrange("p (h d) -> p h d", h=8))
                    self.rope(q3, plain, ri.v(ri.ap[:, 0:512].rearrange("p (h d) -> p h d", h=8)), 8, tmp)
                    k3 = kws[sub].v(kws[sub].ap[:, 0:64].rearrange("p (h d) -> p h d", h=1))
                    self.rope(k3, plain, ri.v(ri.ap[:, 512:576].rearrange("p (h d) -> p h d", h=1)), 1, tmp)
                    pT = pools["pT"].next()
                    for kc in range(4):
                        k.tr(pT[:, kc * 128:(kc + 1) * 128], ri[:, kc * 128:(kc + 1) * 128], identb[:])
                    k.tr(pT[0:64, 512:640], ri[:, 512:576], identb[:])
                    k.copy("act", qiTg[:, :, sub * 128:(sub + 1) * 128], pT.v(pT.ap[:, 0:512].rearrange("p (k t) -> p k t", k=4)))
                    k.copy("act", kiTg[:, sub * 128:(sub + 1) * 128], pT[0:64, 512:640])
                    wi = wir.next()
                    k.ts("dve", wi[:], kws[sub][:, 64:72], float(8 ** -0.5 * 64 ** -0.5), ALU.mult)
                    k.dma("pool", wiS.rv(["g%d" % g], wiS.ap[r0:r0 + 128, :]), wi[:])
                for w_, dst in enumerate((qT, kT)):
                    k.dma("pool", dst.rv(["g%d" % g], dst.ap.rearrange("(kc p) s -> p kc s", p=128)[:, :, t0:t0 + TG]), qTg[w_][:])
                k.dma("pool", qiT.rv(["g%d" % g], qiT.ap.rearrange("(kc p) s -> p kc s", p=128)[:, :, t0:t0 + TG]), qiTg[:])
                k.dma("pool", kiT.rv(["g%d" % g], kiT.ap[:, t0:t0 + TG]), kiTg[:])
            k.barrier()
        k.stack = k.base

    def ds_phase_b1(self, qiT, kiT, wiS, maskT):
        k = self.k
        S, NG, NT = self.S, self.NG, self.NT
        allg = ["g%d" % g for g in range(NG)]
        NIT = 17
        with ExitStack() as ph:
            k.stack = ph
            identb = k.sb("identb", [128, 128], BF16)
            k.dma("sp", identb[:], self.cext["c_identb"][:])
            kiT2 = k.sb("kiT2", [128, S], BF16)
            k.dma("sp", kiT2[0:64, :], kiT.rv(allg, kiT.ap))
            k.dma("sp", kiT2[64:128, :], kiT.rv(allg, kiT.ap))
            scores = [k.sb("score%d" % i, [128, S], F32) for i in range(2)]
            junks = [k.sb("junkb%d" % i, [128, S], BF16) for i in range(2)]
            mk = k.sb("mk", [128, S], BF16)
            qibr = Ring(k, "qib", 2, [128, 4, 128], BF16)
            wibr = Ring(k, "wib", 2, [128, 8], F32)
            rlr = Ring(k, "rl", 4, [128, 512], F32)
            pidx = Ring(k, "pidx", 4, [128, 512], F32, psum=True)
            pTm = Ring(k, "pTm", 2, [128, 1024], BF16, psum=True)
            mTr = Ring(k, "mT", 2, [128, NT, 128], BF16)
            sts = [k.sb("st%d" % i, [128, 8], F32) for i in range(2)]

            def compute_scores(qb, score):
                N = 128 * (qb + 1)
                qib = qibr.next()
                wib = wibr.next()
                k.dma("sp", qib[:], qiT.rv(["g%d" % (qb // 4)], qiT.ap.rearrange("(hp p) s -> p hp s", p=128)[:, :, qb * 128:(qb + 1) * 128]))
                k.dma("sp", wib[:], wiS.rv(["g%d" % (qb // 4)], wiS.ap[qb * 128:(qb + 1) * 128, :]))
                for c0 in range(0, N, 512):
                    cw = min(512, N - c0)
                    for h in range(8):
                        hp, pb = h // 2, 64 * (h % 2)
                        pi = pidx.next()
                        k.mm(pi[:, 0:cw], qib[pb:pb + 64, hp, :], kiT2[pb:pb + 64, c0:c0 + cw])
                        if h == 0:
                            k.ts("dve", score[:, c0:c0 + cw], pi[:, 0:cw], 0.0, ALU.max, wib[:, 0:1], ALU.mult)
                        elif h % 2 == 1:
                            r = rlr.next()
                            k.act(r[:, 0:cw], pi[:, 0:cw], AF.Relu)
                            k.stt(score[:, c0:c0 + cw], r[:, 0:cw], wib[:, h:h + 1], score[:, c0:c0 + cw], ALU.mult, ALU.add)
                        else:
                            r = rlr.next()
                            k.ts("dve", r[:, 0:cw], pi[:, 0:cw], 0.0, ALU.max, wib[:, h:h + 1], ALU.mult)
                            k.tt("pool", score[:, c0:c0 + cw], score[:, c0:c0 + cw], r[:, 0:cw], ALU.add)

            for q0 in range(0, NT, 2):
                qbs = [q for q in (q0, q0 + 1) if q < NT]
                for i, qb in enumerate(qbs):
                    N = 128 * (qb + 1)
                    st = sts[i]
                    compute_scores(qb, scores[i])
                    k.reduce(st[:, 1:2], scores[i][:, 0:N], ALU.max)
                    k.reduce(st[:, 0:1], scores[i][:, 0:N], ALU.min)
                    k.memset("dve", scores[i][0:64, N - 64:N], NEG)
                def tiny(i, qb):
                    N = 128 * (qb + 1)
                    lo, hi, nmid, ssum, pred, d1, d2 = [sts[i][:, j:j + 1] for j in range(7)]
                    k.ts("dve", pred, ssum, float(511 - N), ALU.is_ge)
                    k.stt(d1, nmid, -1.0, lo, ALU.mult, ALU.subtract)
                    k.tt("dve", d2, hi, nmid, ALU.add)
                    k.stt(lo, d1, pred, lo, ALU.mult, ALU.add)
                    k.stt(hi, d2, pred, nmid, ALU.mult, ALU.subtract)

                for it in range(NIT):
                    for i, qb in enumerate(qbs):
                        N = 128 * (qb + 1)
                        lo, hi, nmid, ssum, pred, d1, d2 = [sts[i][:, j:j + 1] for j in range(7)]
                        if it > 0:
                            tiny(i, qb)
                        k.ts("dve", nmid, lo, hi, ALU.add, -0.5, ALU.mult)
                        k.act(junks[i][:, 0:N], scores[i][:, 0:N], AF.Sign, bias=nmid, accum=ssum)
                for i, qb in enumerate(qbs):
                    tiny(i, qb)
                for i, qb in enumerate(qbs):
                    N = 128 * (qb + 1)
                    k.ts("dve", mk[:, 0:N], scores[i][:, 0:N], sts[i][:, 0:1], ALU.is_ge)
                    mT = mTr.next()
                    for s0 in range(0, qb + 1, 8):
                        n = min(8, qb + 1 - s0)
                        pt = pTm.next()
                        for ii in range(n):
                            k.tr(pt[:, ii * 128:(ii + 1) * 128], mk[:, (s0 + ii) * 128:(s0 + ii + 1) * 128], identb[:])
                        k.copy("dve", mT.v(mT.ap[:, s0:s0 + n, :].rearrange("p a b -> p (a b)")), pt[:, 0:n * 128])
                    off = qb * (qb + 1) // 2
                    k.dma("pool", maskT.rv(["q%d" % qb], maskT.ap[:, off:off + qb + 1, :]), mT[:, 0:qb + 1, :])
            k.barrier()
        k.stack = k.base

    def ds_phase_b2(self, qT, kT, vP, maskT, o_d, NHP=8):
        k = self.k
        S, NG, NT = self.S, self.NG, self.NT
        allg = ["g%d" % g for g in range(NG)]
        with ExitStack() as ph:
            k.stack = ph
            kTp = k.sb("kTp", [128, S], BF16)
            qTp = k.sb("qTp", [128, S], BF16)
            vp = k.sb("vp", [128, NT, 128], BF16)
            vaug = k.sb("vaug", [128, NT, 2, 65], BF16)
            mring = Ring(k, "m", 3, [128, NT, 128], BF16)
            lgr = Ring(k, "lg", 3, [128, 512], F32, psum=True)
            oar = Ring(k, "oa", 3, [128, 512], F32, psum=True)
            prr = Ring(k, "p", 5, [128, 512], BF16)
            pmr = Ring(k, "pm", 3, [128, 512], BF16)
            rdr = Ring(k, "rd", 2, [128, 1], F32)
            otr = Ring(k, "ot", 3, [128, 128], BF16)
            cp = [0]
            LAG = 2

            def stage1(job):
                qb, e, c0, n = job["key"]
                pb = 64 * e
                lg = lgr.next()
                for i in range(n):
                    k.mm(lg[:, i * 128:(i + 1) * 128], kTp[pb:pb + 64, (c0 + i) * 128:(c0 + i + 1) * 128],
                         qTp[pb:pb + 64, qb * 128:(qb + 1) * 128])
                p = prr.next()
                k.act(p[:, 0:n * 128], lg[:, 0:n * 128], AF.Exp, scale=0.125)
                job["p"] = p

            def stage2(job):
                qb, e, c0, n = job["key"]
                nst = qb + 1
                m, oa, ot, hp = job["m"], job["oa"], job["ot"], job["hp"]
                pm = pmr.next()
                k.tt("dve" if cp[0] % 2 else "pool", pm[:, 0:n * 128], job["p"][:, 0:n * 128],
                     m.v(m.ap[:, c0:c0 + n, :].rearrange("p a b -> p (a b)")), ALU.mult)
                cp[0] += 1
                for i in range(n):
                    k.mm(oa[:, 0:65], pm[:, i * 128:(i + 1) * 128], vaug[:, c0 + i, e, :],
                         start=(c0 + i == 0), stop=(c0 + i == nst - 1))
                if c0 + n == nst:
                    rd = rdr.next()
                    k.recip(rd[:], oa[:, 64:65])
                    k.ts("dve", ot[:, e * 64:(e + 1) * 64], oa[:, 0:64], rd[:, 0:1], ALU.mult)
                    if e == 1:
                        k.dma("pool", o_d.rv(["g%d" % (qb // 4)], o_d.ap[qb * 128:(qb + 1) * 128, hp * 128:(hp + 1) * 128]), ot[:])

            for hp in range(NHP):
                k.dma("sp", kTp[:], kT.rv(allg, kT.ap[hp * 128:(hp + 1) * 128, :]))
                k.dma("sp", qTp[:], qT.rv(allg, qT.ap[hp * 128:(hp + 1) * 128, :]))
                k.dma("sp", vp[:], vP.rv(allg, vP.ap[hp]))
                k.memset("pool", vaug[:], 1.0)
                k.copy("dve", vaug[:, :, :, 0:64], vp.v(vp.ap.rearrange("p t (e d) -> p t e d", e=2)))
                jobs = []
                for qb in range(NT):
                    nst = qb + 1
                    m = mring.next()
                    off = qb * (qb + 1) // 2
                    ot = otr.next()
                    for e in range(2):
                        oa = oar.next()
                        for c0 in range(0, nst, 4):
                            jobs.append({"key": (qb, e, c0, min(4, nst - c0)), "m": m, "oa": oa, "ot": ot, "hp": hp,
                                         "load": (e == 0 and c0 == 0), "off": off})
                for n in range(len(jobs) + LAG):
                    if n < len(jobs):
                        jb = jobs[n]
                        if jb["load"]:
                            qb = jb["key"][0]
                            k.dma("sp", jb["m"][:, 0:qb + 1, :], maskT.rv(["q%d" % qb], maskT.ap[:, jb["off"]:jb["off"] + qb + 1, :]))
                        stage1(jb)
                    if n >= LAG:
                        stage2(jobs[n - LAG])
            k.barrier()
        k.stack = k.base

    def rw_phase_a1a(self, li, j, xin, raw):
        k = self.k
        S, NG = self.S, self.NG
        with ExitStack() as ph:
            k.stack = ph
            pools = self.front_pools(li, "norm_mix")
            identf = k.sb("identf", [128, 128], F32)
            k.dma("sp", identf[:], self.cext["c_identf"][:])
            mixrow = k.sb("mixrow", [48, 128], F32)
            src = self.ext["rw_mix"]
            k.dma("sp", mixrow[:], src.v(src.ap[j].rearrange("s (kc p) -> (s kc) p", p=128)))
            pacc = Ring(k, "pacc", 4, [128, 512], F32, psum=True)
            mixT = k.sb("mixT", [128, 48], F32)
            pm0 = pacc.next()
            k.tr(pm0[:, 0:48], mixrow[:], identf[0:48, 0:48])
            k.copy("dve", mixT[:], pm0[:, 0:48])
            lor = {}
            specs = [("w", 64, AF.Tanh), ("a", 64, AF.Copy), ("g", 160, AF.Sigmoid)]
            if j >= 1:
                specs.append(("v", 32, AF.Copy))
            for nm, r, fn in specs:
                jj = j - 1 if nm == "v" else j
                w1 = k.sb("l1" + nm, [128, 8, r], BF16)
                k.dma("sp", w1[:], self.wsrc("rw%s1_%d" % (nm, jj), 0, r))
                chunks = []
                dstb = self.wb["rw%s2_%d" % (nm, jj)][0]
                for r0 in range(0, r, 128):
                    rn = min(128, r - r0)
                    w2 = k.sb("l2%s%d" % (nm, r0), [rn, D], BF16)
                    k.dma("sp", w2[:], dstb[r0:r0 + rn, :])
                    t1 = k.sb("t1%s%d" % (nm, r0), [rn, TG], BF16)
                    chunks.append((r0, rn, w2, t1))
                lor[nm] = (w1, chunks, fn)
            xi = Ring(k, "xi", 2, [128, D], F32)
            hTr = Ring(k, "hT", 2, [128, 8, TG + 1], BF16)
            dT = k.sb("dT", [128, 8, TG], BF16)
            xmr = Ring(k, "xm", 2, [128, 8, TG], BF16)
            wr = Ring(k, "w", 3, [128, 8, 512], BF16)
            otr = Ring(k, "ot", 4, [128, 4, 512], BF16)
            cp = [0]

            cur = {}

            def emit(pa, name, g, sub, nb):
                if sub == 0:
                    cur["o"] = otr.next()
                o = cur["o"]
                k.copy("act" if cp[0] % 2 else "dve", o[:, sub, :], pa[:])
                cp[0] += 1
                if sub == 3:
                    t0_ = g * TG
                    dst = raw[name]
                    keys = ["t%d" % (t0_ // 128 + i) for i in range(4)]
                    k.dma("pool", dst.rv(keys, dst.ap[t0_:t0_ + TG, nb * 512:(nb + 1) * 512].rearrange("(s p) c -> p s c", p=128)), o[:])

            def lora(nm, xm, g):
                w1, chunks, fn = lor[nm]
                for (r0, rn, w2, t1) in chunks:
                    p1 = pacc.next()
                    for kc in range(8):
                        k.mm(p1[0:rn, :], w1[:, kc, r0:r0 + rn], xm[:, kc, :], start=(kc == 0), stop=(kc == 7))
                    k.act(t1[:], p1[0:rn, :], fn)
                for nb in range(2):
                    for sub in range(4):
                        pa = pacc.next()
                        for ci, (r0, rn, w2, t1) in enumerate(chunks):
                            k.mm(pa[:], t1[0:rn, sub * 128:(sub + 1) * 128], w2[0:rn, nb * 512:(nb + 1) * 512],
                                 start=(ci == 0), stop=(ci == len(chunks) - 1))
                        emit(pa, "l" + nm, g, sub, nb)

            hprev = None
            for g in range(NG):
                t0 = g * TG
                hT = hTr.next()
                if hprev is None:
                    k.memset("pool", hT[:, :, 0:1], 0.0)
                else:
                    k.copy("pool", hT[:, :, 0:1], hprev[:, :, TG:TG + 1])
                for sub in range(4):
                    r0 = t0 + sub * 128
                    xt = xi.next()
                    k.dma("sp", xt[:], xin.rv(["t%d" % (r0 // 128)], xin.ap[r0:r0 + 128, :]))
                    self.norm_transpose(xt[:], pools["gam"][:], hT, sub, pools, col0=1)
                hprev = hT
                k.tt("dve", dT[:], hT[:, :, 0:TG], hT[:, :, 1:TG + 1], ALU.subtract)
                for s, name in enumerate(("r", "k", "v", "w", "a", "g")):
                    xm = xmr.next()
                    for kc in range(8):
                        k.stt(xm[:, kc, :], dT[:, kc, :], mixT[:, s * 8 + kc:s * 8 + kc + 1], hT[:, kc, 1:TG + 1], ALU.mult, ALU.add)
                    if s < 3:
                        for nb in range(2):
                            w = wr.next()
                            k.dma("sp", w[:], self.wsrc("rwrkv%d_%d" % (j, s), nb * 512, 512))
                            for sub in range(4):
                                pa = pacc.next()
                                for kc in range(8):
                                    k.mm(pa[:], xm[:, kc, sub * 128:(sub + 1) * 128], w[:, kc, :], start=(kc == 0), stop=(kc == 7))
                                emit(pa, name, g, sub, nb)
                        if s == 2 and j >= 1:
                            lora("v", xm, g)
                    else:
                        lora(name, xm, g)
            k.barrier()
        k.stack = k.base

    def rw_phase_a1b(self, li, j, raw, tok, feat, glt_d, vfirst):
        k = self.k
        S, NT = self.S, self.NT
        NC = S // 64
        with ExitStack() as ph:
            k.stack = ph
            identf = k.sb("identf", [128, 128], F32)
            k.dma("sp", identf[:], self.cext["c_identf"][:])
            tribd = k.sb("tribd", [128, 128], F32)
            onesbd = k.sb("onesbd", [128, 128], F32)
            k.dma("sp", tribd[:], self.cext["c_tribd"][:])
            k.dma("sp", onesbd[:], self.cext["c_onesbd"][:])
            prm = {}
            plist = [("w0", "rw_w0", j), ("a0", "rw_a0", j), ("kk", "rw_k_k", j), ("ka", "rw_k_a", j)]
            if j >= 1:
                plist.append(("v0", "rw_v0", j - 1))
            for nm, ext, jj in plist:
                t = k.sb("prm_" + nm, [128, D], F32)
                src = self.ext[ext]
                k.dma("sp", t[:], src.v(src.ap[jj:jj + 1, :].partition_broadcast(128)))
                prm[nm] = t
            t = k.sb("prm_rk", [128, D], F32)
            src = self.ext["rw_r_k"]
            k.dma("sp", t[:], src.v(src.ap[j].rearrange("h d -> (h d)").unsqueeze(0).partition_broadcast(128)))
            prm["rk"] = t
            glt = k.sb("glt", [128, 8, NC], F32)
            names_in = ["r", "k", "v", "lw", "la"] + (["lv"] if j >= 1 else [])
            inr = {n: Ring(k, "in_" + n, 3, [128, 512], BF16) for n in names_in}
            if j >= 1:
                inr["vf"] = Ring(k, "in_vf", 3, [128, 512], BF16)
            T = {n: Ring(k, "t_" + n, 2, [128, 512], F32) for n in
                 ["vw", "a", "ld", "kk", "kp", "b", "t1", "t2", "cs", "g", "gi", "gp", "gr", "gl",
                  "R", "P", "B", "K", "Bb", "Kb", "BV"]}
            sm = Ring(k, "sm", 2, [128, 4, 8], F32)
            ps_c = Ring(k, "psc", 2, [128, 512], F32, psum=True)
            ps_t = Ring(k, "pst", 2, [128, 512], F32, psum=True)
            ps_x = Ring(k, "psx", 3, [128, 512], F32, psum=True)
            xtr = Ring(k, "xT", 3, [128, 4, 128], F32)
            c24 = k.sb("c24", [128, 1], F32)
            k.memset("dve", c24[:], 0.0)
            def load_tiles(st, nb):
                r0 = st * 128
                cs_ = slice(nb * 512, (nb + 1) * 512)
                L = {}
                for n in names_in:
                    t = inr[n].next()
                    k.dma("sp", t[:], raw[n].rv(["t%d" % st], raw[n].ap[r0:r0 + 128, cs_]))
                    L[n] = t
                if j >= 1:
                    t = inr["vf"].next()
                    k.dma("sp", t[:], vfirst.rv(["t%d" % st], vfirst.ap[r0:r0 + 128, cs_]))
                    L["vf"] = t
                return L

            order = [(st, nb) for st in range(NT) for nb in range(2)]
            pending = load_tiles(*order[0])
            for oi, (st, nb) in enumerate(order):
                if True:
                    r0 = st * 128
                    cs_ = slice(nb * 512, (nb + 1) * 512)
                    L = pending
                    if oi + 1 < len(order):
                        pending = load_tiles(*order[oi + 1])
                    a, ld, kk, kp, b, t1, t2 = [T[n].next() for n in ("a", "ld", "kk", "kp", "b", "t1", "t2")]
                    s4 = sm.next()
                    k.tt("dve", a[:], L["la"][:], prm["a0"][:, cs_], ALU.add)
                    k.act(a[:], a[:], AF.Sigmoid)
                    k.tt("pool", ld[:], L["lw"][:], prm["w0"][:, cs_], ALU.add)
                    k.act(ld[:], ld[:], AF.Sigmoid)
                    k.ts("pool", ld[:], ld[:], -0.6065306597126334, ALU.mult)
                    v = T["vw"].next()
                    k.copy("pool", v[:], L["v"][:])
                    if j >= 1:
                        k.tt("dve", t1[:], L["lv"][:], prm["v0"][:, cs_], ALU.add)
                        k.act(t1[:], t1[:], AF.Sigmoid)
                        k.tt("dve", t2[:], L["vf"][:], v[:], ALU.subtract)
                        k.tt("dve", t2[:], t2[:], t1[:], ALU.mult)
                        k.tt("dve", v[:], v[:], t2[:], ALU.add)
                    else:
                        k.dma("sp", vfirst.rv(["t%d" % st], vfirst.ap[r0:r0 + 128, cs_]), L["v"][:])
                    k.dma("sp", tok["V"].rv(["t%d" % st], tok["V"].ap[r0:r0 + 128, cs_]), v[:])
                    k.tt("dve", kk[:], L["k"][:], prm["kk"][:, cs_], ALU.mult)
                    k.act(t1[:], kk[:], AF.Square)
                    k.reduce(s4[:, 0, :], t1.v(t1.ap.rearrange("p (h d) -> p h d", h=8)), ALU.add)
                    k.ts("dve", s4[:, 0, :], s4[:, 0, :], 1e-24, ALU.max)
                    k.act(s4[:, 1, :], s4[:, 0, :], AF.Sqrt)
                    k.recip(s4[:, 2, :], s4[:, 1, :])
                    kk3 = kk.v(kk.ap.rearrange("p (h d) -> p h d", h=8))
                    k.tt("dve", kk3, kk3, V(s4.v(s4.ap).bufs, s4.ap[:, 2, :].unsqueeze(2).to_broadcast([128, 8, 64])), ALU.mult)
                    k.stt(t2[:], a[:], -1.0, prm["ka"][:, cs_], ALU.add, ALU.mult)
                    k.stt(kp[:], t2[:], 1.0, L["k"][:], ALU.add, ALU.mult)
                    k.tt("pool", b[:], kk[:], a[:], ALU.mult)
                    k.tt("dve", t1[:], L["r"][:], kp[:], ALU.mult)
                    k.tt("dve", t1[:], t1[:], prm["rk"][:, cs_], ALU.mult)
                    k.reduce(s4[:, 3, :], t1.v(t1.ap.rearrange("p (h d) -> p h d", h=8)), ALU.add)
                    bv = T["BV"].next()
                    k.tt("dve", bv.v(bv.ap.rearrange("p (h d) -> p h d", h=8)), v.v(v.ap.rearrange("p (h d) -> p h d", h=8)),
                         V(s4.v(s4.ap).bufs, s4.ap[:, 3, :].unsqueeze(2).to_broadcast([128, 8, 64])), ALU.mult)
                    k.dma("sp", tok["BV"].rv(["t%d" % st], tok["BV"].ap[r0:r0 + 128, cs_]), bv[:])
                    pc = ps_c.next()
                    pt = ps_t.next()
                    k.mm(pc[:], tribd[:], ld[:])
                    k.mm(pt[:], onesbd[:], ld[:])
                    cs, g_, gi, gp, gr, gl = [T[n].next() for n in ("cs", "g", "gi", "gp", "gr", "gl")]
                    k.copy("dve", cs[:], pc[:])
                    k.act(g_[:], pc[:], AF.Exp)
                    k.act(gi[:], pc[:], AF.Exp, scale=-1.0)
                    k.tt("dve", gp[:], cs[:], ld[:], ALU.subtract)
                    k.act(gp[:], gp[:], AF.Exp)
                    k.tt("dve", gr[:], pt[:], cs[:], ALU.subtract)
                    k.act(gr[:], gr[:], AF.Exp)
                    k.act(gl[:], pt[:], AF.Exp)
                    R, P, B, Kt, Bb, Kb = [T[n].next() for n in ("R", "P", "B", "K", "Bb", "Kb")]
                    k.tt("dve", R[:], L["r"][:], g_[:], ALU.mult)
                    k.stt(P[:], kk[:], -1.0, gp[:], ALU.mult, ALU.mult)
                    k.tt("pool", B[:], b[:], gi[:], ALU.mult)
                    k.tt("pool", Kt[:], kp[:], gi[:], ALU.mult)
                    k.tt("pool", Bb[:], b[:], gr[:], ALU.mult)
                    k.tt("dve", Kb[:], kp[:], gr[:], ALU.mult)
                    k.dma("sp", tok["P"].rv(["t%d" % st], tok["P"].ap[r0:r0 + 128, cs_]), P[:])
                    k.dma("sp", tok["Bb"].rv(["t%d" % st], tok["Bb"].ap[r0:r0 + 128, cs_]), Bb[:])
                    k.dma("sp", tok["Kb"].rv(["t%d" % st], tok["Kb"].ap[r0:r0 + 128, cs_]), Kb[:])
                    for nm, src_t in (("P", P), ("R", R), ("B", B), ("K", Kt)):
                        px = ps_x.next()
                        for pr in range(4):
                            k.tr(px[:, pr * 128:(pr + 1) * 128], src_t[:, pr * 128:(pr + 1) * 128], identf[:])
                        xt = xtr.next()
                        k.copy("act", xt[:], px.v(px.ap.rearrange("p (a t) -> p a t", a=4)))
                        dst = feat[nm]
                        k.dma("sp", dst.rv(["t%d" % st], dst.ap.rearrange("(a p) s -> p a s", p=128)[:, nb * 4:(nb + 1) * 4, r0:r0 + 128]), xt[:])
                    px = ps_x.next()
                    for pr in range(4):
                        k.tr(px[:, pr * 128:(pr + 1) * 128], gl[:, pr * 128:(pr + 1) * 128], identf[:])
                    k.copy("act", glt[:, nb * 4:(nb + 1) * 4, 2 * st:2 * st + 2],
                           px.v(px.ap.rearrange("p (a c t) -> p a c t", a=4, c=2)[:, :, :, 0]))
            k.dma("pool", glt_d[:], glt[:])
            k.barrier()
        k.stack = k.base

    def rw_phase_b(self, tok, feat, glt_d, y_d):
        k = self.k
        S, NT = self.S, self.NT
        NC = S // 64
        with ExitStack() as ph:
            k.stack = ph
            cm = {}
            for nm in ("c_id64", "c_mst", "c_mit", "c_ms"):
                t = k.sb(nm, [128, 64], F32)
                k.dma("sp", t[:], self.cext[nm][:])
                cm[nm] = t
            glt = k.sb("glt", [128, 8, NC], F32)
            k.dma("sp", glt[:], glt_d[:])
            tkr = {n: Ring(k, "tk_" + n, 2, [128, D], F32) for n in ("V", "P", "Bb", "Kb")}
            ftr = {n: Ring(k, "ft_" + n, 2, [128, 8, 128], F32) for n in ("P", "R", "B", "K")}
            tkb = {n: k.sb("tkb_" + n, [128, D], BF16) for n in ("V", "P", "Bb", "Kb")}
            AT = {n: k.sb("AT_" + n, [128, 16, 64], (BF16 if n in ("rb", "pk", "rk") else F32)) for n in ("pb", "rb", "pk", "rk", "s0")}
            SN = {n: Ring(k, "nm_" + n, 2, [128, 8, 64], F32) for n in ("S", "N")}
            QT = k.sb("QT", [128, 16, 64], BF16)
            Qr = Ring(k, "Qr", 2, [128, 8, 64], F32)
            W1 = k.sb("W1", [128, 16, 64], BF16)
            U0 = k.sb("U0", [128, 16, 64], BF16)
            PH = k.sb("PH", [128, 16, 64], BF16)
            DG = k.sb("DG", [128, 2, 8, 64], F32)
            MT = k.sb("MT", [128, 2, 8, 64], F32)
            NCs = k.sb("NCs", [128, 2, 8, 64], F32)
            RHT = k.sb("RHT", [128, 2, 8, 64], BF16)
            Hbr = Ring(k, "Hb", 2, [128, 8, 64], BF16)
            Y0 = k.sb("Y0", [128, 16, 64], F32)
            Hr = Ring(k, "H", 2, [128, 8, 64], F32)
            ysb = Ring(k, "ysb", 2, [128, D], F32)
            PP = [k.ps("PP%d" % i, [128, 1024], F32) for i in range(4)]

            def bc(buf, n):
                return V((buf,), buf.ap.unsqueeze(1).to_broadcast([128, n, 64]))

            def v16(pp):
                return pp.v(pp.ap.rearrange("p (h t) -> p h t", h=16))

            def v28(pp):
                return pp.v(pp.ap.rearrange("p (c a t) -> p c a t", c=2, a=8))

            HE = [(hp, e, 2 * hp + e, e * 8 + hp) for e in range(2) for hp in range(8)]
            def load_st(st):
                r0 = st * 128
                tk_, ft_ = {}, {}
                for n in tkr:
                    tk_[n] = tkr[n].next()
                    k.dma("sp", tk_[n][:], tok[n].rv(["t%d" % st], tok[n].ap[r0:r0 + 128, :]))
                for n in ftr:
                    ft_[n] = ftr[n].next()
                    k.dma("sp", ft_[n][:], feat[n].rv(["t%d" % st], feat[n].ap.rearrange("(a p) s -> p a s", p=128)[:, :, r0:r0 + 128]))
                return tk_, ft_

            H = Hr.next()
            k.memset("dve", H[:], 0.0)
            Hb = Hbr.next()
            k.memset("pool", Hb[:], 0.0)
            nxt = load_st(0)
            for st in range(NT):
                r0 = st * 128
                tk, ft = nxt
                if st + 1 < NT:
                    nxt = load_st(st + 1)
                for ci, n in enumerate(("V", "P", "Bb", "Kb")):
                    k.copy("pool" if ci % 2 else "act", tkb[n][:], tk[n][:])
                blocks = [("pb", "B", "P", "c_mst"), ("s0", "P", "B", "c_ms"), ("rb", "B", "R", "c_mit"),
                          ("pk", "K", "P", "c_mst"), ("rk", "K", "R", "c_mit")]
                for bi, (nm, ln, rn, mk) in enumerate(blocks):
                    pp = PP[bi % 2]
                    for hp, e, h, hi in HE:
                        kb = 64 * e
                        for c in range(2):
                            ob = 64 * c
                            k.mm(pp[ob:ob + 64, hi * 64:(hi + 1) * 64],
                                 ft[ln][kb:kb + 64, hp, 64 * c:64 * c + 64], ft[rn][kb:kb + 64, hp, 64 * c:64 * c + 64])
                    k.tt("dve", AT[nm][:], v16(pp), bc(cm[mk], 16), ALU.mult)
                for hb in range(2):
                    Sk = AT["s0"][:, hb * 8:(hb + 1) * 8, :]
                    Nk = AT["pb"][:, hb * 8:(hb + 1) * 8, :]
                    Q = Qr.next()
                    k.tt("dve", Q[:], Nk, bc(cm["c_id64"], 8), ALU.add)
                    for lev in range(5):
                        for hh in range(8):
                            for c in range(2):
                                ob = 64 * c
                                k.mm(PP[2][ob:ob + 64, hh * 64:(hh + 1) * 64],
                                     V(Nk.bufs, Nk.ap[ob:ob + 64, hh, :]), V(Sk.bufs, Sk.ap[ob:ob + 64, hh, :]))
                        if lev < 4:
                            for hh in range(8):
                                for c in range(2):
                                    ob = 64 * c
                                    k.mm(PP[2][ob:ob + 64, 512 + hh * 64:512 + (hh + 1) * 64],
                                         V(Sk.bufs, Sk.ap[ob:ob + 64, hh, :]), V(Nk.bufs, Nk.ap[ob:ob + 64, hh, :]))
                        Sn = SN["S"].next()
                        k.copy("act", Sn[:], PP[2].v(PP[2].ap[:, 0:512].rearrange("p (h t) -> p h t", h=8)))
                        if lev < 4:
                            Nn = SN["N"].next()
                            k.copy("dve", Nn[:], PP[2].v(PP[2].ap[:, 512:1024].rearrange("p (h t) -> p h t", h=8)))
                        for hh in range(8):
                            for c in range(2):
                                ob = 64 * c
                                k.mm(PP[3][ob:ob + 64, hh * 64:(hh + 1) * 64], Sn[ob:ob + 64, hh, :], Q[ob:ob + 64, hh, :])
                        if lev < 4:
                            Qn = Qr.next()
                            qdst = Qn[:]
                        else:
                            Qn = None
                            qdst = QT[:, hb * 8:(hb + 1) * 8, :]
                        k.tt("dve", qdst, PP[3].v(PP[3].ap[:, 0:512].rearrange("p (h t) -> p h t", h=8)), Q[:], ALU.add)
                        Sk = Sn[:]
                        if lev < 4:
                            Nk = Nn[:]
                            Q = Qn
                for hp, e, h, hi in HE:
                    for c in range(2):
                        ob = 64 * c
                        k.mm(PP[0][ob:ob + 64, hi * 64:(hi + 1) * 64], AT["pk"][ob:ob + 64, hi, :], tkb["V"][ob:ob + 64, h * 64:(h + 1) * 64])
                k.copy("act", W1[:], v16(PP[0]))
                for hp, e, h, hi in HE:
                    for c in range(2):
                        ob = 64 * c
                        k.mm(PP[1][ob:ob + 64, hi * 64:(hi + 1) * 64], QT[ob:ob + 64, hi, :], W1[ob:ob + 64, hi, :])
                        k.mm(PP[0][ob:ob + 64, hi * 64:(hi + 1) * 64], QT[ob:ob + 64, hi, :], tkb["P"][ob:ob + 64, h * 64:(h + 1) * 64])
                k.copy("act", U0[:], v16(PP[1]))
                k.copy("dve", PH[:], v16(PP[0]))
                k.tt("dve", DG[:], V((cm["c_id64"],), cm["c_id64"].ap.unsqueeze(1).unsqueeze(1).to_broadcast([128, 2, 8, 64])),
                     V((glt,), glt.ap[:, :, 2 * st:2 * st + 2].rearrange("p a c -> p c a").unsqueeze(3).to_broadcast([128, 2, 8, 64])), ALU.mult)
                for hp, e, h, hi in HE:
                    kb = 64 * e
                    hs = slice(h * 64, (h + 1) * 64)
                    for c in range(2):
                        ob = 64 * c
                        sl = c * 512 + hp * 64
                        k.mm(PP[1][kb:kb + 64, sl:sl + 64], PH[ob:ob + 64, hi, :], tkb["Bb"][ob:ob + 64, hs])
                        k.mm(PP[2][kb:kb + 64, sl:sl + 64], tkb["Bb"][ob:ob + 64, hs], U0[ob:ob + 64, hi, :], start=True, stop=False)
                        k.mm(PP[2][kb:kb + 64, sl:sl + 64], tkb["Kb"][ob:ob + 64, hs], tkb["V"][ob:ob + 64, hs], start=False, stop=True)
                        k.mm(PP[3][kb:kb + 64, sl:sl + 64], PH[ob:ob + 64, hi, :], AT["rb"][ob:ob + 64, hi, :])
                        k.mm(PP[0][ob:ob + 64, hi * 64:(hi + 1) * 64], AT["rb"][ob:ob + 64, hi, :], U0[ob:ob + 64, hi, :], start=True, stop=False)
                        k.mm(PP[0][ob:ob + 64, hi * 64:(hi + 1) * 64], AT["rk"][ob:ob + 64, hi, :], tkb["V"][ob:ob + 64, hs], start=False, stop=True)
                k.tt("dve", MT[:], v28(PP[1]), DG[:], ALU.add)
                k.copy("act", NCs[:], v28(PP[2]))
                k.tt("dve", RHT[:], v28(PP[3]), V((ft["R"],), ft["R"].ap.rearrange("p a (c t) -> p c a t", c=2)), ALU.add)
                k.copy("act", Y0[:], v16(PP[0]))
                yt = ysb.next()
                for c in range(2):
                    ob = 64 * c
                    for hp, e, h, hi in HE:
                        kb = 64 * e
                        k.mm(PP[1][ob:ob + 64, hi * 64:(hi + 1) * 64], RHT[kb:kb + 64, c, hp, :], Hb[kb:kb + 64, hp, :])
                    k.tt("dve", yt.v(yt.ap[ob:ob + 64, :].rearrange("p (a e d) -> p e a d", e=2, d=64)),
                         PP[1].v(PP[1].ap[ob:ob + 64, :].rearrange("p (e a d) -> p e a d", e=2, d=64)),
                         Y0.v(Y0.ap[ob:ob + 64].rearrange("p (e a) d -> p e a d", e=2)), ALU.add)
                    for hp, e, h, hi in HE:
                        kb = 64 * e
                        k.mm(PP[2][kb:kb + 64, hp * 64:(hp + 1) * 64], MT[kb:kb + 64, c, hp, :], H[kb:kb + 64, hp, :])
                    Hn = Hr.next()
                    k.tt("dve", Hn[:], PP[2].v(PP[2].ap[:, 0:512].rearrange("p (a t) -> p a t", a=8)), NCs[:, c, :, :], ALU.add)
                    H = Hn
                    Hb = Hbr.next()
                    k.copy("act", Hb[:], Hn[:])
                k.dma("pool", y_d.rv(["t%d" % st], y_d.ap[r0:r0 + 128, :]), yt[:])
            k.barrier()
        k.stack = k.base

    def rw_phase_post(self, j, y_d, raw_g, bv_d, o_d):
        k = self.k
        S, NT = self.S, self.NT
        with ExitStack() as ph:
            k.stack = ph
            prm = {}
            for nm, ext in (("lw", "rw_ln_w"), ("lb", "rw_ln_b")):
                t = k.sb("prm_" + nm, [128, D], F32)
                src = self.ext[ext]
                k.dma("sp", t[:], src.v(src.ap[j:j + 1, :].partition_broadcast(128)))
                prm[nm] = t
            geps = k.sb("geps", [128, 1], F32)
            k.memset("dve", geps[:], GN_EPS)
            yr = Ring(k, "y", 2, [128, D], F32)
            gr = Ring(k, "g", 2, [128, D], BF16)
            br = Ring(k, "bv", 2, [128, D], F32)
            jr = Ring(k, "jk", 1, [128, D], F32)
            sr = Ring(k, "s", 2, [128, 6, 16], F32)
            obr = Ring(k, "ob", 2, [128, D], BF16)
            for st in range(NT):
                r0 = st * 128
                y, g, bv = yr.next(), gr.next(), br.next()
                k.dma("sp", y[:], y_d.rv(["t%d" % st], y_d.ap[r0:r0 + 128, :]))
                k.dma("sp", g[:], raw_g.rv(["t%d" % st], raw_g.ap[r0:r0 + 128, :]))
                k.dma("sp", bv[:], bv_d.rv(["t%d" % st], bv_d.ap[r0:r0 + 128, :]))
                s = sr.next()
                jk = jr.next()
                y3 = y.v(y.ap.rearrange("p (h d) -> p h d", h=16))
                k.reduce(s[:, 0, :], y3, ALU.add)
                k.act(jk[:], y[:], AF.Square)
                k.reduce(s[:, 1, :], jk.v(jk.ap.rearrange("p (h d) -> p h d", h=16)), ALU.add)
                k.ts("dve", s[:, 2, :], s[:, 0, :], 1.0 / 64, ALU.mult)
                k.tt("dve", s[:, 3, :], s[:, 2, :], s[:, 2, :], ALU.mult)
                k.stt(s[:, 4, :], s[:, 1, :], 1.0 / 64, s[:, 3, :], ALU.mult, ALU.subtract)
                k.act(s[:, 5, :], s[:, 4, :], AF.Sqrt, bias=geps[:, 0:1])
                k.recip(s[:, 5, :], s[:, 5, :])
                bcs = lambda i: V(s.v(s.ap).bufs, s.ap[:, i, :].unsqueeze(2).to_broadcast([128, 16, 64]))
                k.tt("dve", y3, y3, bcs(2), ALU.subtract)
                k.tt("dve", y3, y3, bcs(5), ALU.mult)
                k.tt("pool", y[:], y[:], prm["lw"][:], ALU.mult)
                k.tt("pool", y[:], y[:], prm["lb"][:], ALU.add)
                k.tt("dve", y[:], y[:], bv[:], ALU.add)
                o = obr.next()
                k.tt("dve", o[:], y[:], g[:], ALU.mult)
                k.dma("pool", o_d.rv(["g%d" % (st // 4)], o_d.ap[r0:r0 + 128, :]), o[:])
            k.barrier()
        k.stack = k.base

    def build(self):
        k = self.k
        S = self.S
        for li, (kind, j) in enumerate(self.kinds):
            self.cast_weight("gu%d" % li, self.ext["ffn_w_gu"].ap[li], D, 2 * FF)
            self.cast_weight("dn%d" % li, self.ext["ffn_w_down"].ap[li], FF, D)
            if kind == 1:
                self.cast_weight("sbqkv", self.ext["sb_w_qkv"].ap[j], D, 3 * D)
                self.cast_weight("sbout", self.ext["sb_w_out"].ap[j], D, D)
            if kind == 0:
                for s_ in range(3):
                    self.cast_weight("rwrkv%d_%d" % (j, s_), self.ext["rw_w_rkv"].ap[j, s_], D, D)
                self.cast_weight("rwout%d" % j, self.ext["rw_w_out"].ap[j], D, D)
                for nm, r in (("w", 64), ("a", 64), ("g", 160)):
                    self.cast_weight("rw%s1_%d" % (nm, j), self.ext["rw_%s1" % nm].ap[j], D, r)
                    self.cast_weight("rw%s2_%d" % (nm, j), self.ext["rw_%s2" % nm].ap[j], r, D)
                if j >= 1:
                    self.cast_weight("rwv1_%d" % (j - 1), self.ext["rw_v1"].ap[j - 1], D, 32)
                    self.cast_weight("rwv2_%d" % (j - 1), self.ext["rw_v2"].ap[j - 1], 32, D)
            if kind == 2:
                self.cast_weight("dsin", self.ext["ds_w_in"].ap[j], D, 3656)
                self.cast_weight("dsout", self.ext["ds_w_out"].ap[j], D, D)
        xin = self.ext["x"]
        rws = None
        for li, (kind, j) in enumerate(self.kinds):
            if kind == 0:
                if rws is None:
                    rws = {"raw": {n: k.dram("rw_raw_" + n, [S, D], BF16) for n in ("r", "k", "v", "lw", "la", "lg", "lv")},
                           "tok": {n: k.dram("rw_tok_" + n, [S, D], F32) for n in ("V", "P", "Bb", "Kb", "BV")},
                           "feat": {n: k.dram("rw_feat_" + n, [D, S], F32) for n in ("P", "R", "B", "K")},
                           "glt": k.dram("rw_glt", [128, 8, S // 64], F32),
                           "vfirst": k.dram("rw_vfirst", [S, D], BF16),
                           "y": k.dram("rw_y", [S, D], F32),
                           "o": k.dram("rw_o", [S, D], BF16)}
                stop = 9
                self.rw_phase_a1a(li, j, xin, rws["raw"])
                if stop >= 1:
                    self.rw_phase_a1b(li, j, rws["raw"], rws["tok"], rws["feat"], rws["glt"], rws["vfirst"])
                if stop >= 2:
                    self.rw_phase_b(rws["tok"], rws["feat"], rws["glt"], rws["y"])
                if stop >= 3:
                    self.rw_phase_post(j, rws["y"], rws["raw"]["lg"], rws["tok"]["BV"], rws["o"])
                if stop >= 4:
                    self.phase_c(li, rws["o"], "t", "rwout%d" % j, xin)
            if kind == 1:
                qT = k.dram("sb_qT", [D, S], BF16)
                kT = k.dram("sb_kT", [D, S], BF16)
                vP = k.dram("sb_vP", [8, 128, self.NT, 128], BF16)
                oT = k.dram("sb_oT", [D, S], BF16)
                self.sb_phase_a(li, j, xin, qT, kT, vP)
                self.sb_phase_b(qT, kT, vP, oT)
                self.phase_c(li, oT, "f", "sbout", xin)
            if kind == 2:
                NT = self.NT
                qT = k.dram("ds_qT", [D, S], BF16)
                kT = k.dram("ds_kT", [D, S], BF16)
                vP = k.dram("ds_vP", [8, 128, NT, 128], BF16)
                qiT = k.dram("ds_qiT", [512, S], BF16)
                kiT = k.dram("ds_kiT", [64, S], BF16)
                wiS = k.dram("ds_wi", [S, 8], F32)
                maskT = k.dram("ds_maskT", [128, NT * (NT + 1) // 2, 128], BF16)
                o_d = k.dram("ds_o", [S, D], BF16)
                self.ds_phase_a(li, j, xin, qT, kT, vP, qiT, kiT, wiS)
                self.ds_phase_b1(qiT, kiT, wiS, maskT)
                self.ds_phase_b2(qT, kT, vP, maskT, o_d)
                self.phase_c(li, o_d, "t", "dsout", xin)
            xin = self.y
        k.barrier()
        print("program: ninst=%d nwait=%d" % (k.ninst, k.nwait))
        return self.nc


W_NAMES = ["norm_mix", "norm_ffn", "ffn_w_gu", "ffn_w_down", "rw_mix", "rw_w_rkv", "rw_w0", "rw_w1", "rw_w2",
           "rw_a0", "rw_a1", "rw_a2", "rw_g1", "rw_g2", "rw_v0", "rw_v1", "rw_v2", "rw_k_k", "rw_k_a", "rw_r_k",
           "rw_ln_w", "rw_ln_b", "rw_w_out", "sb_w_qkv", "sb_w_out", "ds_w_in", "ds_q_norm", "ds_k_norm", "ds_w_out"]


def run_model(inputs, kinds):
    x = np.ascontiguousarray(inputs["x"], dtype=np.float32)
    B, S, _ = x.shape
    wspecs = {"x": (S, D)}
    for n in W_NAMES:
        wspecs[n] = tuple(inputs[n].shape)
    prog = Prog(S, kinds, wspecs)
    nc = prog.build()
    consts = Prog.const_values(S)
    shared = {n: np.ascontiguousarray(inputs[n], dtype=np.float32) for n in W_NAMES}
    shared.update(consts)
    in_maps = []
    for c in range(8):
        m = dict(shared)
        m["x"] = x[(c // 2) % B]
        in_maps.append(m)
    res = run_bass_kernel_spmd(nc, in_maps, core_ids=list(range(8)))
    out = np.stack([np.asarray(res.results[2 * b]["y"], dtype=np.float32) for b in range(B)] if B == 4 else
                   [np.asarray(res.results[(2 * b) % 8]["y"], dtype=np.float32) for b in range(B)])
    return out


def kernel(**inputs):
    kinds = [(i % 3, i // 3) for i in range(4)]
    return run_model(inputs, kinds)
```
